# Optimizing a Trainium2 kernel written in Bass

```python
import jax, jax.numpy as jnp
from jax import lax
import numpy as np

D_MODEL = 4096
BATCH = 4
SEQ = 4096
DEPTH = 1

HEAD_DIM = 128
D_MIX = D_MODEL
W_A = D_MIX // 2
W_B = D_MIX - W_A
N_GROUPS_A = W_A // HEAD_DIM
N_GROUPS_B = W_B // HEAD_DIM
CONV_A = 3
CONV_B = 31
D_IN = 4 * W_A + 3 * W_B
SPLITS = (W_A, 2 * W_A, 3 * W_A, 4 * W_A, 4 * W_A + W_B, 4 * W_A + 2 * W_B)
EPS = 1e-6

kernel_name = "hymba_style_conv_hybrid_adaln"


def _rmsnorm(x, g):
    xf = x.astype(jnp.float32)
    y = xf * lax.rsqrt(jnp.mean(xf * xf, axis=-1, keepdims=True) + EPS)
    return (y * g.astype(jnp.float32)).astype(x.dtype)


def _layernorm(x, g, b):
    xf = x.astype(jnp.float32)
    mu = jnp.mean(xf, axis=-1, keepdims=True)
    xc = xf - mu
    var = jnp.mean(xc * xc, axis=-1, keepdims=True)
    y = xc * lax.rsqrt(var + EPS) * g.astype(jnp.float32) + b.astype(jnp.float32)
    return y.astype(x.dtype)


def _causal_depthwise_conv(u, w):
    k, ch = w.shape
    return lax.conv_general_dilated(
        u, w[:, None, :].astype(u.dtype),
        window_strides=(1,), padding=((k - 1, 0),),
        dimension_numbers=("NWC", "WIO", "NWC"),
        feature_group_count=ch)


def setup_inputs(seed: int = 0) -> dict:
    key = jax.random.key(seed)
    ks = jax.random.split(key, 14)
    f32 = jnp.float32
    x = jax.random.normal(ks[0], (BATCH, SEQ, D_MODEL), f32)
    c = jax.random.normal(ks[1], (BATCH, D_MODEL), f32)
    norm_g = 1.0 + 0.01 * jax.random.normal(ks[2], (DEPTH, D_MODEL), f32)
    w_ada = 0.5 * D_MODEL ** -0.5 * jax.random.normal(ks[3], (DEPTH, D_MODEL, 3 * D_MODEL), f32)
    b_ada = 0.01 * jax.random.normal(ks[4], (DEPTH, 3 * D_MODEL), f32)
    w_in = D_MODEL ** -0.5 * jax.random.normal(ks[5], (DEPTH, D_MODEL, D_IN), f32)
    conv_a_w = CONV_A ** -0.5 * jax.random.normal(ks[6], (DEPTH, CONV_A, W_A), f32)
    conv_b_w = CONV_B ** -0.5 * jax.random.normal(ks[7], (DEPTH, CONV_B, W_B), f32)
    conv_b_b = 0.01 * jax.random.normal(ks[8], (DEPTH, W_B), f32)
    ln_b_g = 1.0 + 0.01 * jax.random.normal(ks[9], (DEPTH, W_B), f32)
    ln_b_b = 0.01 * jax.random.normal(ks[10], (DEPTH, W_B), f32)
    w_out = D_MIX ** -0.5 * jax.random.normal(ks[11], (DEPTH, D_MIX, D_MODEL), f32)
    final_g = 1.0 + 0.01 * jax.random.normal(ks[12], (D_MODEL,), f32)
    return {"x": x, "c": c, "norm_g": norm_g, "w_ada": w_ada, "b_ada": b_ada,
            "w_in": w_in, "conv_a_w": conv_a_w, "conv_b_w": conv_b_w,
            "conv_b_b": conv_b_b, "ln_b_g": ln_b_g, "ln_b_b": ln_b_b,
            "w_out": w_out, "final_g": final_g}


def reference(x, c, norm_g, w_ada, b_ada, w_in, conv_a_w, conv_b_w, conv_b_b,
              ln_b_g, ln_b_b, w_out, final_g):
    c_act = jax.nn.silu(c)
    for l in range(DEPTH):
        mod = c_act @ w_ada[l] + b_ada[l]
        shift, scale, gate = jnp.split(mod, 3, axis=-1)
        h = _rmsnorm(x, norm_g[l]) * (1.0 + scale[:, None, :]) + shift[:, None, :]

        proj = jnp.einsum("bsd,de->bse", h, w_in[l])
        a_b, a_c, a_x, a_z, b_v, b_g, b_z = jnp.split(proj, SPLITS, axis=-1)

        y_a = a_b * _causal_depthwise_conv(a_c * a_x, conv_a_w[l]) * jax.nn.silu(a_z)

        u = b_v * jax.nn.sigmoid(b_g)
        u = _causal_depthwise_conv(u, conv_b_w[l]) + conv_b_b[l]
        y_b = jax.nn.silu(_layernorm(u, ln_b_g[l], ln_b_b[l])) * jax.nn.silu(b_z)

        y = jnp.concatenate([y_a, y_b], axis=-1)
        x = x + gate[:, None, :] * jnp.einsum("bse,ed->bsd", y, w_out[l])
    return _rmsnorm(x, final_g)
```

```python
from contextlib import ExitStack
import numpy as np
import concourse.bass as bass
import concourse.mybir as mybir
from concourse.bass_utils import run_bass_kernel_spmd

F32 = mybir.dt.float32
BF16 = mybir.dt.bfloat16
ALU = mybir.AluOpType
AF = mybir.ActivationFunctionType
AX = mybir.AxisListType
EPS = 1e-6
CONV_A = 3
CONV_B = 31
HALO = 32
T = 512


class Tok:
    __slots__ = ("sem", "sid", "val")

    def __init__(self, sem, sid, val):
        self.sem = sem
        self.sid = sid
        self.val = val


class _Eng:
    def __init__(self, name, sem, sid):
        self.name = name
        self.sem = sem
        self.sid = sid
        self.count = 0
        self.ops = []
        self.waited = {}


class Sched:
    ENGS = ("pe", "act", "dve", "pool", "sp")

    def __init__(self, nc, stack, n_dma_sems=(("pool", 8), ("sp", 8), ("act", 6))):
        self.nc = nc
        self.e = {}
        self._sid = 0
        for n in self.ENGS:
            sem = stack.enter_context(nc.semaphore("s_" + n))
            self.e[n] = _Eng(n, sem, self._next_sid())
        self.dma_sems = {}
        self._rr = {}
        for q, n in n_dma_sems:
            self.dma_sems[q] = []
            self._rr[q] = 0
            for i in range(n):
                sem = stack.enter_context(nc.semaphore("d_%s%d" % (q, i)))
                self.dma_sems[q].append([sem, self._next_sid(), 0, None])
        self.last_w = {}
        self.readers = {}
        self.out_toks = []

    def _next_sid(self):
        self._sid += 1
        return self._sid

    def _deps(self, reads, writes):
        deps = []
        for k in reads:
            t = self.last_w.get(k)
            if t is not None:
                deps.append(t)
        for k in writes:
            t = self.last_w.get(k)
            if t is not None:
                deps.append(t)
            deps.extend(self.readers.get(k, ()))
        return deps

    def _waits(self, E, deps):
        best = {}
        for t in deps:
            if t is None:
                continue
            if E.waited.get(t.sid, 0) >= t.val:
                continue
            if t.sid not in best or best[t.sid].val < t.val:
                best[t.sid] = t
        for sid, t in best.items():
            E.waited[sid] = t.val
        return list(best.values())

    def _record(self, tok, reads, writes):
        for k in writes:
            self.last_w[k] = tok
            self.readers[k] = []
        for k in reads:
            self.readers.setdefault(k, []).append(tok)

    def op(self, eng, fn, reads=(), writes=()):
        E = self.e[eng]
        waits = self._waits(E, self._deps(reads, writes))
        E.count += 1
        tok = Tok(E.sem, E.sid, E.count)
        E.ops.append((waits, fn, E.sem, 1))
        self._record(tok, reads, writes)
        return tok

    def dma(self, queue, fn, reads=(), writes=(), is_output=False):
        E = self.e[queue]
        sl = self.dma_sems[queue]
        slot = sl[self._rr[queue]]
        self._rr[queue] = (self._rr[queue] + 1) % len(sl)
        deps = self._deps(reads, writes)
        if slot[3] is not None:
            deps.append(slot[3])
        waits = self._waits(E, deps)
        slot[2] += 16
        tok = Tok(slot[0], slot[1], slot[2])
        slot[3] = tok
        E.ops.append((waits, fn, slot[0], 16))
        self._record(tok, reads, writes)
        if is_output:
            self.out_toks.append(tok)
        return tok

    def finish(self, eng="sp"):
        E = self.e[eng]
        waits = self._waits(E, self.out_toks)
        E.ops.append((waits, None, None, 0))

    def emit(self, block):
        def run(E, h):
            for waits, fn, sem, inc in E.ops:
                for t in waits:
                    h.wait_ge(t.sem, t.val)
                if fn is not None:
                    fn(h).then_inc(sem, inc)

        if self.e["pe"].ops:
            @block.tensor
            def _(h):
                run(self.e["pe"], h)
        if self.e["act"].ops:
            @block.scalar
            def _(h):
                run(self.e["act"], h)
        if self.e["dve"].ops:
            @block.vector
            def _(h):
                run(self.e["dve"], h)
        if self.e["pool"].ops:
            @block.gpsimd
            def _(h):
                run(self.e["pool"], h)
        if self.e["sp"].ops:
            @block.sync
            def _(h):
                run(self.e["sp"], h)


def build_program(D, NTOK):
    KC = D // 128
    WA = D // 2
    CA = WA // 128
    CB = CA
    DIN = 7 * WA
    NT = NTOK // T
    NQ = D // 512
    KH = KC // 2
    SEG_AB, SEG_AC, SEG_AX, SEG_AZ = 0, WA, 2 * WA, 3 * WA
    SEG_BV, SEG_BG, SEG_BZ = 4 * WA, 5 * WA, 6 * WA
    NBUF = 3
    NGEN = 7
    NBANK = 5

    nc = bass.Bass("TRN2", target_bir_lowering=False)

    def din(name, shape):
        return nc.dram_tensor(name, shape, F32, kind="ExternalInput")

    xT_t = din("xT", [D, NTOK])
    xTh_t = din("xTh", [D, HALO])
    x_t = din("x", [NTOK, D])
    c_t = din("c_pp", [128, KC])
    wada_t = din("w_ada", [D, 3 * D])
    bss_t = din("b_ss_pp", [128, 2 * KC])
    bgate_t = din("bgate_bc", [128, D])
    ng_t = din("ng_pp", [128, KC])
    win_t = din("w_in", [D, DIN])
    wout_t = din("w_out", [D, D])
    wA_t = din("wA_pp", [128, CA * CONV_A])
    wB_t = din("wB_pp", [128, CB * CONV_B])
    bB_t = din("bB_pp", [128, CB])
    lng_t = din("lng_pp", [128, CB])
    lnb_t = din("lnb_pp", [128, CB])
    fg_t = din("fg_bc", [128, D])
    hm_t = din("hmask", [128, 1])
    out_t = nc.dram_tensor("out", [NTOK, D], F32, kind="ExternalOutput")
    gate_d = nc.dram_tensor("gate_d", [128, D], F32)

    xT_v = xT_t.ap().rearrange("(kc p) t -> p kc t", p=128)
    xTh_v = xTh_t.ap().rearrange("(kc p) t -> p kc t", p=128)
    wada_v = wada_t.ap().rearrange("(kc p) e -> p kc e", p=128)
    win_v = win_t.ap().rearrange("(kc p) e -> p kc e", p=128)
    wout_v = wout_t.ap().rearrange("(ec p) d -> p ec d", p=128)
    x_ap = x_t.ap()
    out_ap = out_t.ap()
    gate_ap = gate_d.ap()

    with ExitStack() as st:
        S = Sched(nc, st)

        def sb(name, shape, dt=F32):
            return st.enter_context(nc.sbuf_tensor(name, shape, dt))

        slab = [sb("slab%d" % i, [128, KC, 256], BF16) for i in range(NBUF)]
        hx = sb("hx", [128, 2 * D], F32)
        h = hx.bitcast(BF16).reshape([128, KC, T])
        y = sb("y", [128, KC, T], BF16)
        big = sb("big", [128, 2 * D], F32)
        gf = sb("gf", [128, 2, 512], F32)
        xa = [sb("xa%d" % i, [128, 2, T], F32) for i in range(2)]
        gen_t = [sb("gen%d" % i, [128, T], F32) for i in range(NGEN)]
        cxb = [sb("cx%d" % i, [128, T + 2], F32) for i in range(2)]
        convo = [sb("convo%d" % i, [128, T], F32) for i in range(2)]
        ubuf = [sb("ub%d" % i, [128, T + 30], F32) for i in range(2)]
        acc1 = sb("acc1", [128, T], F32)
        vsum = sb("vsum", [128, T], F32)
        qsum = sb("qsum", [128, T], F32)
        accss = sb("accss", [128, T], F32)
        rstd1 = sb("rstd1", [128, T], F32)
        lnr = sb("lnr", [128, T], F32)
        lnm = sb("lnm", [128, T], F32)
        hh = sb("hh", [128, KC, HALO], BF16)
        cxc = sb("cxc", [128, CA, 2], F32)
        ucar = sb("ucar", [128, CB, 30], F32)
        ssq = sb("ssq", [128, 4, 8], F32)
        sst = sb("sst", [128, 4], F32)
        cpp = sb("cpp", [128, KC], F32)
        mod = sb("mod", [128, 2 * KC], F32)
        bss = sb("bss", [128, 2 * KC], F32)
        ngp = sb("ngp", [128, KC], F32)
        gs = sb("gs", [128, KC], F32)
        wA = sb("wA", [128, CA * CONV_A], F32)
        wB = sb("wB", [128, CB * CONV_B], F32)
        bB = sb("bB", [128, CB], F32)
        lng = sb("lng", [128, CB], F32)
        lnb = sb("lnb", [128, CB], F32)
        hm = sb("hm", [128, 1], F32)
        ones_f = sb("ones_f", [128, 128], F32)
        ones_b = sb("ones_b", [128, 128], BF16)
        xh_all = bass.AP(xa[0], 0, [[2 * T, 128], [HALO, KC], [1, HALO]])
        sqh_all = bass.AP(xa[1], 0, [[2 * T, 128], [HALO, KC], [1, HALO]])

        def xh_kc(kc):
            return bass.AP(xa[0], kc * HALO, [[2 * T, 128], [1, HALO]])

        psum = [st.enter_context(nc.psum_tensor("ps%d" % i, [128, 512], F32)) for i in range(8)]
        PS_MISC, PS_S1, PS_S2 = 5, 6, 7

        def vbuf(c):
            return big[:, c * T:(c + 1) * T]

        def xr(jj):
            return big[:, jj * D:(jj + 1) * D] if jj < 2 else hx[:, (jj - 2) * D:(jj - 1) * D]

        def xr_slice(jj, q):
            t_, o = (big, jj) if jj < 2 else (hx, jj - 2)
            return t_[:, o * D + q * 512: o * D + (q + 1) * 512]

        def xr_key(jj, q):
            return ("big", jj * (D // 512) + q) if jj < 2 else ("hx", jj - 2, q)

        def xr_keys(jj):
            return [xr_key(jj, q) for q in range(D // 512)]

        def h_keys(kc):
            ks = [("hx", (kc * 256) // D, ((kc * 256) % D) // 512)]
            if kc == KC - 1:
                ks.append("hrd")
            return ks


        state = {"gen": 0, "bank": 0, "slab_issued": 0, "slab_used": 0}

        def gen():
            i = state["gen"]
            state["gen"] = (i + 1) % NGEN
            return gen_t[i], ("gen", i)

        def next_bank():
            b = state["bank"]
            state["bank"] = (b + 1) % NBANK
            return b

        def act(out, in_, func, reads, writes, **kw):
            return S.op("act", lambda e: e.activation(out=out, in_=in_, func=func, **kw), reads, writes)

        def tt(out, in0, in1, op, reads, writes):
            return S.op("dve", lambda e: e.tensor_tensor(out=out, in0=in0, in1=in1, op=op), reads, writes)

        def ts(out, in0, s1, s2, op0, op1, reads, writes):
            if s2 is None:
                return S.op("dve", lambda e: e.tensor_scalar(out=out, in0=in0, scalar1=s1, scalar2=None,
                                                             op0=op0), reads, writes)
            return S.op("dve", lambda e: e.tensor_scalar(out=out, in0=in0, scalar1=s1, scalar2=s2,
                                                         op0=op0, op1=op1), reads, writes)

        def stt(out, in0, scalar, in1, op0, op1, reads, writes):
            return S.op("dve", lambda e: e.scalar_tensor_tensor(out=out, in0=in0, scalar=scalar, in1=in1,
                                                                op0=op0, op1=op1), reads, writes)

        def cp(out, in_, reads, writes):
            return S.op("dve", lambda e: e.tensor_copy(out=out, in_=in_), reads, writes)

        def mm(out, pairs, reads, writes, start=True, stop=True):
            def fn(e):
                ins = None
                n = len(pairs)
                for i, (l, r) in enumerate(pairs):
                    ins = e.matmul(out, lhsT=l, rhs=r, start=(start and i == 0), stop=(stop and i == n - 1))
                return ins
            return S.op("pe", fn, reads, writes)

        def load(queue, out, in_, writes, reads=()):
            return S.dma(queue, lambda e: e.dma_start(out=out, in_=in_), reads=reads, writes=writes)

        def rsqrt_inplace(ap, key):
            act(ap, ap, AF.Sqrt, [], [key])
            S.op("dve", lambda e: e.reciprocal(out=ap, in_=ap), [], [key])

        NSLAB_T = 3 * CA + CB // 2 + 2 * NQ
        wcache = nc.dram_tensor("wcache", [NSLAB_T, 128, KC, 256], BF16)
        wc_ap = wcache.ap()
        slabs = []
        for s in range(KC):
            slabs.append(("w", [(0, 256, wada_v, s * 256)], None, "plain"))
        for s in range(D // 256):
            slabs.append(("w", [(0, 256, wada_v, 2 * D + s * 256)], None, "plain"))
        for i in range(NT):
            cid = 0

            def wmode(cid, i=i):
                fill_tile = 0 if (cid % 3 == 0 or NT == 1) else 1
                return "plain" if i < fill_tile else ("fill" if i == fill_tile else "cached")
            for c in range(CA):
                slabs.append(("w", [(0, 128, win_v, SEG_AC + c * 128), (128, 128, win_v, SEG_AX + c * 128)], cid, wmode(cid)))
                slabs.append(("w", [(0, 128, win_v, SEG_AZ + c * 128), (128, 128, win_v, SEG_AB + c * 128)], cid + 1, wmode(cid + 1)))
                slabs.append(("w", [(0, 128, win_v, SEG_BV + c * 128), (128, 128, win_v, SEG_BG + c * 128)], cid + 2, wmode(cid + 2)))
                cid += 3
            for c2 in range(CB // 2):
                slabs.append(("w", [(0, 256, win_v, SEG_BZ + c2 * 256)], cid, wmode(cid)))
                cid += 1
            for q in range(NQ):
                for eh in range(2):
                    slabs.append(("o", (q, eh), cid + 2 * q + eh, "fill" if i == 0 else "cached"))

        def slab_o_view(b):
            return bass.AP(slab[b], 0, [[KC * 256, 128], [512, KH], [1, 512]])

        BOTH = lambda b: [("slab", b, 0), ("slab", b, 1)]

        def issue_slab(n):
            b = n % NBUF
            kind, payload, cid, mode = slabs[n]
            if mode == "cached":
                S.dma("pool", lambda e, b=b, cid=cid: e.dma_start(out=slab[b][:], in_=wc_ap[cid]),
                      reads=[("wc", cid)], writes=BOTH(b))
                return
            if kind == "w":
                for (off, ncol, src_v, col) in payload:
                    keys = [("slab", b, off // 128)] if ncol == 128 else BOTH(b)
                    S.dma("pool", lambda e, b=b, off=off, ncol=ncol, src_v=src_v, col=col:
                          e.dma_start(out=slab[b][:, :, off:off + ncol], in_=src_v[:, :, col:col + ncol]),
                          writes=keys)
            else:
                q, eh = payload
                S.dma("pool", lambda e, b=b, q=q, eh=eh:
                      e.dma_start(out=slab_o_view(b), in_=wout_v[:, eh * KH:(eh + 1) * KH, q * 512:(q + 1) * 512]),
                      writes=BOTH(b))

        def writeback_slab(n):
            b = n % NBUF
            kind, payload, cid, mode = slabs[n]
            if mode == "fill":
                S.dma("pool", lambda e, b=b, cid=cid: e.dma_start(out=wc_ap[cid], in_=slab[b][:]),
                      reads=BOTH(b), writes=[("wc", cid)])

        def next_slab():
            n = state["slab_used"]
            while state["slab_issued"] < min(len(slabs), n + NBUF):
                k = state["slab_issued"]
                issue_slab(k)
                if k >= 1:
                    writeback_slab(k - 1)
                state["slab_issued"] += 1
            state["slab_used"] = n + 1
            return n % NBUF

        S.op("dve", lambda e: e.memset(ones_f[:], 1.0), [], ["ones_f"])
        S.op("dve", lambda e: e.memset(ones_b[:], 1.0), [], ["ones_b"])
        for (dst, src, key) in ((cpp, c_t, "cpp"), (bss, bss_t, "bss"), (ngp, ng_t, "ngp"), (wA, wA_t, "wA"),
                                (wB, wB_t, "wB"), (bB, bB_t, "bB"), (lng, lng_t, "lng"), (lnb, lnb_t, "lnb"),
                                (hm, hm_t, "hm")):
            load("sp", dst[:], src.ap(), [key])
        ts(wB[:], wB[:], 0.5, None, ALU.mult, None, [], ["wB"])

        cact = sb("cact", [128, KC], F32)
        cact_b = sb("cact_b", [128, KC], BF16)
        shift = mod

        def prologue():
            s1g0 = stage1_p1(0, PS_S1)
            stage1_halo_p1(PS_S2)
            _prologue_body(s1g0)
            pump(s1g0, 10 ** 6)

        def _prologue_body(s1g0):
          if True:
            act(cact[:], cpp[:], AF.Silu, ["cpp"], ["cact"])
            cp(cact_b[:], cact[:], ["cact"], ["cact_b"])
            for kc in range(KC):
                ts(y[:, kc, 0:128], ones_b[:], cact[:, kc:kc + 1], None, ALU.mult, None,
                   ["ones_b", "cact"], [("y", kc)])
            for s in range(KC):
                b = next_slab()
                for jj in range(2):
                    j = 2 * s + jj
                    pairs = [(slab[b][:, kc, jj * 128:(jj + 1) * 128], cact_b[:, kc:kc + 1]) for kc in range(KC)]
                    mm(psum[PS_MISC][:, j:j + 1], pairs, [("slab", b, 0), ("slab", b, 1), "cact_b"], [("ps", PS_MISC)])
                pump(s1g0, 1)
            tt(mod[:], psum[PS_MISC][:, 0:2 * KC], bss[:], ALU.add, ["bss"], ["mod", ("ps", PS_MISC)])
            stt(gs[:], mod[:, KC:2 * KC], 1.0, ngp[:], ALU.add, ALU.mult, ["mod", "ngp"], ["gs"])
            for s in range(D // 256):
                b = next_slab()
                bank = next_bank()
                pairs = [(y[:, kc, 0:128], slab[b][:, kc, :]) for kc in range(KC)]
                mm(psum[bank][:, 0:256], pairs, [("slab", b, 0), ("slab", b, 1)] + [("y", kc) for kc in range(KC)],
                   [("ps", bank)])
                g, gk = gen()
                bg, bgk = gen()
                load("sp", bg[:, 0:256], bgate_t.ap()[:, s * 256:(s + 1) * 256], [bgk])
                tt(g[:, 0:256], psum[bank][:, 0:256], bg[:, 0:256], ALU.add, [bgk], [gk, ("ps", bank)])
                S.dma("sp", lambda e, g=g, s=s: e.dma_start(out=gate_ap[:, s * 256:(s + 1) * 256], in_=g[:, 0:256]),
                      reads=[gk], writes=["gate_d"])

        def stage1_p1(i, bank=None):
            bank = PS_MISC if bank is None else bank
            t0 = i * T
            for g2 in range(KC // 2):
                xi = g2 % 2
                load("sp", xa[xi][:], xT_v[:, 2 * g2:2 * g2 + 2, t0:t0 + T], [("xa", xi)])
                for s in range(2):
                    kc = 2 * g2 + s
                    if kc == 0:
                        act(accss[:], xa[xi][:, s, :], AF.Square, [("xa", xi)], ["accss"])
                    else:
                        sq, sqk = gen()
                        act(sq[:], xa[xi][:, s, :], AF.Square, [("xa", xi)], [sqk])
                        tt(accss[:], accss[:], sq[:], ALU.add, [sqk], ["accss"])
                yield
            mm(psum[bank][:, 0:T], [(ones_f[:], accss[:])], ["ones_f", "accss"], [("ps", bank)])
            ts(rstd1[:], psum[bank][:, 0:T], 1.0 / D, EPS, ALU.mult, ALU.add, [], ["rstd1", ("ps", bank)])
            rsqrt_inplace(rstd1[:], "rstd1")
            yield

        def stage1_p2(i):
            t0 = i * T
            for g2 in range(KC // 2):
                xi = g2 % 2
                load("sp", xa[xi][:], xT_v[:, 2 * g2:2 * g2 + 2, t0:t0 + T], [("xa", xi)])
                for s in range(2):
                    kc = 2 * g2 + s
                    tmp, tk = gen()
                    stt(tmp[:], xa[xi][:, s, :], gs[:, kc:kc + 1], rstd1[:], ALU.mult, ALU.mult,
                        [("xa", xi), "gs", "rstd1"], [tk])
                    act(h[:, kc, :], tmp[:], AF.Identity, [tk, "mod"], h_keys(kc), bias=shift[:, kc:kc + 1])

        def pump(g, n):
            for _ in range(n):
                if g is None:
                    return
                try:
                    next(g)
                except StopIteration:
                    return

        def stage1_halo_p1(bank):
            load("sp", xh_all, xTh_v, [("xa", 0)])
            act(sqh_all, xh_all, AF.Square, [("xa", 0)], [("xa", 1)])
            S.op("dve", lambda e: e.tensor_reduce(out=accss[:, 0:HALO],
                                                  in_=bass.AP(xa[1], 0, [[2 * T, 128], [1, HALO], [HALO, KC]]),
                                                  axis=AX.X, op=ALU.add), [("xa", 1)], ["accss"])
            mm(psum[bank][:, 0:HALO], [(ones_f[:], accss[:, 0:HALO])], ["ones_f", "accss"], [("ps", bank)])
            ts(lnr[:, 0:HALO], psum[bank][:, 0:HALO], 1.0 / D, EPS, ALU.mult, ALU.add, [], ["lnr", ("ps", bank)])
            rsqrt_inplace(lnr[:, 0:HALO], "lnr")

        def stage1_halo_p2():
            load("sp", xh_all, xTh_v, [("xa", 0)])
            for kc in range(KC):
                tmp, tk = gen()
                stt(tmp[:, 0:HALO], xh_kc(kc), gs[:, kc:kc + 1], lnr[:, 0:HALO], ALU.mult, ALU.mult,
                    [("xa", 0), "gs", "lnr"], [tk])
                act(hh[:, kc, :], tmp[:, 0:HALO], AF.Identity, [tk, "mod"], ["hh"], bias=shift[:, kc:kc + 1])

        def proj_group(b, piece, halo):
            bank = next_bank()
            pairs = [(slab[b][:, kc, piece * 128:(piece + 1) * 128], h[:, kc, :]) for kc in range(KC)]
            mm(psum[bank][:, 0:T], pairs, [("slab", b, piece), "hrd"], [("ps", bank)])
            hb = None
            if halo:
                hb = next_bank()
                pairs = [(slab[b][:, kc, piece * 128:(piece + 1) * 128], hh[:, kc, :]) for kc in range(KC)]
                mm(psum[hb][:, 0:HALO], pairs, [("slab", b, piece), "hh"], [("ps", hb)])
            return bank, hb

        def mixA_1(i, c, b_ac, h_ac, b_ax, h_ax):
            if i == 0:
                a, ak = gen()
                act(a[:, 0:HALO], psum[h_ac][:, 0:HALO], AF.Copy, [], [ak, ("ps", h_ac)])
                t_, tk = gen()
                tt(t_[:, 0:HALO], psum[h_ax][:, 0:HALO], a[:, 0:HALO], ALU.mult, [ak], [tk, ("ps", h_ax)])
                ts(cxc[:, c, :], t_[:, HALO - 2:HALO], hm[:, 0:1], None, ALU.mult, None, [tk, "hm"], [("cxc", c)])
            a, ak = gen()
            act(a[:], psum[b_ac][:, 0:T], AF.Copy, [], [ak, ("ps", b_ac)])
            cx = cxb[c % 2]
            ck = ("cx", c % 2)
            cp(cx[:, 0:2], cxc[:, c, :], [("cxc", c)], [ck])
            tt(cx[:, 2:T + 2], psum[b_ax][:, 0:T], a[:], ALU.mult, [ak], [ck, ("ps", b_ax)])
            cp(cxc[:, c, :], cx[:, T:T + 2], [ck], [("cxc", c)])
            co = convo[c % 2]
            cok = ("convo", c % 2)
            t_, tk = gen()
            w = lambda k: wA[:, c * CONV_A + k:c * CONV_A + k + 1]
            ts(co[:], cx[:, 0:T], w(0), None, ALU.mult, None, [ck, "wA"], [cok])
            stt(t_[:], cx[:, 1:T + 1], w(1), co[:], ALU.mult, ALU.add, [ck, "wA", cok], [tk])
            stt(co[:], cx[:, 2:T + 2], w(2), t_[:], ALU.mult, ALU.add, [ck, "wA", tk], [cok])

        def mixA_2(i, c, b_az, b_ab):
            sz, szk = gen()
            act(sz[:], psum[b_az][:, 0:T], AF.Silu, [], [szk, ("ps", b_az)])
            t4, t4k = gen()
            tt(t4[:], psum[b_ab][:, 0:T], convo[c % 2][:], ALU.mult, [("convo", c % 2)], [t4k, ("ps", b_ab)])
            tt(y[:, c, :], t4[:], sz[:], ALU.mult, [t4k, szk], [("y", c)])

        def mixB_1(i, c, b_bv, h_bv, b_bg, h_bg):
            if i == 0:
                th, thk = gen()
                act(th[:, 0:HALO], psum[h_bg][:, 0:HALO], AF.Tanh, [], [thk, ("ps", h_bg)], scale=0.5)
                u_, uk = gen()
                stt(u_[:, 0:HALO], th[:, 0:HALO], 1.0, psum[h_bv][:, 0:HALO], ALU.add, ALU.mult,
                    [thk], [uk, ("ps", h_bv)])
                ts(ucar[:, c, :], u_[:, HALO - 30:HALO], hm[:, 0:1], None, ALU.mult, None, [uk, "hm"], [("ucar", c)])
            th, thk = gen()
            act(th[:], psum[b_bg][:, 0:T], AF.Tanh, [], [thk, ("ps", b_bg)], scale=0.5)
            ub = ubuf[c % 2]
            ubk = ("ub", c % 2)
            cp(ub[:, 0:30], ucar[:, c, :], [("ucar", c)], [ubk])
            stt(ub[:, 30:T + 30], th[:], 1.0, psum[b_bv][:, 0:T], ALU.add, ALU.mult, [thk], [ubk, ("ps", b_bv)])
            cp(ucar[:, c, :], ub[:, T:T + 30], [ubk], [("ucar", c)])
            vk = ("big", c)
            bufs = [(vbuf(c), vk), (acc1[:], "acc1")]
            w = lambda k: wB[:, c * CONV_B + k:c * CONV_B + k + 1]
            ts(vbuf(c), ub[:, 0:T], w(0), bB[:, c:c + 1], ALU.mult, ALU.add, [ubk, "wB", "bB"], [vk])
            for k in range(1, CONV_B):
                o, ok = bufs[k % 2]
                p, pk = bufs[(k - 1) % 2]
                stt(o, ub[:, k:k + T], w(k), p, ALU.mult, ALU.add, [ubk, "wB", pk], [ok])

        def mixB_stats(c):
            if c == 0:
                cp(vsum[:], vbuf(c), [("big", c)], ["vsum"])
                act(qsum[:], vbuf(c), AF.Square, [("big", c)], ["qsum"])
            else:
                sq, sqk = gen()
                act(sq[:], vbuf(c), AF.Square, [("big", c)], [sqk])
                tt(vsum[:], vsum[:], vbuf(c), ALU.add, [("big", c)], ["vsum"])
                tt(qsum[:], qsum[:], sq[:], ALU.add, [sqk], ["qsum"])
            if c == CB - 1:
                mm(psum[PS_S1][:, 0:T], [(ones_f[:], vsum[:])], ["ones_f", "vsum"], [("ps", PS_S1)])
                mm(psum[PS_S2][:, 0:T], [(ones_f[:], qsum[:])], ["ones_f", "qsum"], [("ps", PS_S2)])

        def ln_finalize():
            mean, mk = gen()
            act(mean[:], psum[PS_S1][:, 0:T], AF.Copy, [], [mk, ("ps", PS_S1)], scale=1.0 / WA)
            msq, qk = gen()
            tt(msq[:], mean[:], mean[:], ALU.mult, [mk], [qk])
            stt(lnr[:], psum[PS_S2][:, 0:T], 1.0 / WA, msq[:], ALU.mult, ALU.subtract, [qk], ["lnr", ("ps", PS_S2)])
            ts(lnr[:], lnr[:], EPS, None, ALU.add, None, [], ["lnr"])
            rsqrt_inplace(lnr[:], "lnr")
            stt(lnm[:], mean[:], -1.0, lnr[:], ALU.mult, ALU.mult, [mk, "lnr"], ["lnm"])

        def mixB_2(c, b_bz):
            t1, k1 = gen()
            tt(t1[:], vbuf(c), lnr[:], ALU.mult, [("big", c), "lnr"], [k1])
            t2, k2 = gen()
            tt(t2[:], t1[:], lnm[:], ALU.add, [k1, "lnm"], [k2])
            s1, sk = gen()
            act(s1[:], t2[:], AF.Silu, [k2, "lng", "lnb"], [sk], scale=lng[:, c:c + 1], bias=lnb[:, c:c + 1])
            sz, szk = gen()
            act(sz[:], psum[b_bz][:, 0:T], AF.Silu, [], [szk, ("ps", b_bz)])
            tt(y[:, CA + c, :], s1[:], sz[:], ALU.mult, [sk, szk], [("y", CA + c)])

        def outproj_tile(i, s1g, npump):
            NB = T // 128
            order = [2, 3, 0, 1]
            rows = [i * T + jj * 128 for jj in range(NB)]
            for jj in (0, 1):
                load("sp", xr(jj), x_ap[rows[jj]:rows[jj] + 128, :], xr_keys(jj))
            for q in range(NQ):
                load("sp", gf[:, 0, :], gate_ap[:, q * 512:(q + 1) * 512], ["gf0"], reads=["gate_d"])
                load("sp", gf[:, 1, :], fg_t.ap()[:, q * 512:(q + 1) * 512], ["gf1"])
                if q == 0:
                    for jj in (2, 3):
                        load("sp", xr(jj), x_ap[rows[jj]:rows[jj] + 128, :], xr_keys(jj) + ["hrd"])
                banks = {jj: next_bank() for jj in order}
                for eh in range(2):
                    b = next_slab()
                    for jj in order:
                        tcol = jj * 128
                        pairs = [(y[:, eh * KH + e, tcol:tcol + 128],
                                  bass.AP(slab[b], e * 512, [[KC * 256, 128], [1, 512]])) for e in range(KH)]
                        mm(psum[banks[jj]][:, 0:512], pairs,
                           [("slab", b, 0), ("slab", b, 1)] + [("y", eh * KH + e) for e in range(KH)],
                           [("ps", banks[jj])],
                           start=(eh == 0), stop=(eh == 1))
                for jj in order:
                    kq = xr_key(jj, q)
                    xs = xr_slice(jj, q)
                    tmp, tk = gen()
                    tt(tmp[:], psum[banks[jj]][:, 0:512], gf[:, 0, :], ALU.mult, ["gf0"], [tk, ("ps", banks[jj])])
                    tt(xs, tmp[:], xs, ALU.add, [tk], [kq])
                    sq, sqk = gen()
                    act(sq[:], xs, AF.Square, [kq], [sqk, ("ssq", jj)], accum_out=ssq[:, jj, q:q + 1])
                    tt(xs, xs, gf[:, 1, :], ALU.mult, ["gf1"], [kq])
                pump(s1g, npump)
            pump(s1g, 10 ** 6)
            def rstd_pair(j0):
                key = "sst%d" % j0
                for jj in (j0, j0 + 1):
                    S.op("dve", lambda e, jj=jj: e.tensor_reduce(out=sst[:, jj:jj + 1], in_=ssq[:, jj, 0:NQ],
                                                                axis=AX.X, op=ALU.add), [("ssq", jj)], [key])
                ts(sst[:, j0:j0 + 2], sst[:, j0:j0 + 2], 1.0 / D, EPS, ALU.mult, ALU.add, [], [key])
                rsqrt_inplace(sst[:, j0:j0 + 2], key)

            def fin(jj):
                act(xr(jj), xr(jj), AF.Copy, ["sst%d" % (2 * (jj // 2))], xr_keys(jj), scale=sst[:, jj:jj + 1])
                S.dma("act", lambda e, jj=jj: e.dma_start(out=out_ap[rows[jj]:rows[jj] + 128, :], in_=xr(jj)),
                      reads=xr_keys(jj), is_output=True)
            rstd_pair(2)
            fin(2)
            fin(3)
            if i + 1 < NT:
                stage1_p2(i + 1)
            rstd_pair(0)
            fin(0)
            fin(1)

        prologue()
        stage1_halo_p2()
        stage1_p2(0)
        for i in range(NT):
            halo = (i == 0)
            pend_stats = None
            for c in range(CA):
                b = next_slab()
                b_ac, h_ac = proj_group(b, 0, halo)
                b_ax, h_ax = proj_group(b, 1, halo)
                mixA_1(i, c, b_ac, h_ac, b_ax, h_ax)
                b = next_slab()
                b_az, _ = proj_group(b, 0, False)
                b_ab, _ = proj_group(b, 1, False)
                mixA_2(i, c, b_az, b_ab)
                b = next_slab()
                b_bv, h_bv = proj_group(b, 0, halo)
                b_bg, h_bg = proj_group(b, 1, halo)
                if pend_stats is not None:
                    mixB_stats(pend_stats)
                mixB_1(i, c, b_bv, h_bv, b_bg, h_bg)
                pend_stats = c
            bz_banks = {}
            for c2 in range(CB // 2):
                b = next_slab()
                for s in range(2):
                    bz_banks[2 * c2 + s], _ = proj_group(b, s, False)
                if c2 == 0:
                    mixB_stats(pend_stats)
                    ln_finalize()
                for s in range(2):
                    mixB_2(2 * c2 + s, bz_banks[2 * c2 + s])
            s1g = stage1_p1(i + 1) if i + 1 < NT else None
            outproj_tile(i, s1g, -(-(KC // 2 + 1) // NQ))
        S.finish("act")
        with nc.Block() as block:
            S.emit(block)
    return nc


def make_in_maps(x, c, norm_g, w_ada, b_ada, w_in, conv_a_w, conv_b_w, conv_b_b, ln_b_g, ln_b_b, w_out,
                 final_g, n_cores):
    B, SEQ, D = x.shape
    per_b = n_cores // B
    NTOK = SEQ // per_b
    KC = D // 128
    WA = D // 2
    CA = WA // 128
    f32 = np.float32

    def pp(v, n):
        return np.ascontiguousarray(np.asarray(v, f32).reshape(n, 128).T)

    w_ada0 = np.ascontiguousarray(np.asarray(w_ada[0], f32))
    w_in0 = np.ascontiguousarray(np.asarray(w_in[0], f32))
    w_out0 = np.ascontiguousarray(np.asarray(w_out[0], f32))
    b_ada0 = np.asarray(b_ada[0], f32)
    shared = {
        "w_ada": w_ada0, "w_in": w_in0, "w_out": w_out0,
        "b_ss_pp": pp(b_ada0[0:2 * D], 2 * KC),
        "bgate_bc": np.ascontiguousarray(np.broadcast_to(b_ada0[2 * D:3 * D][None, :], (128, D))),
        "ng_pp": pp(norm_g[0], KC),
        "wA_pp": np.ascontiguousarray(np.asarray(conv_a_w[0], f32).reshape(CONV_A, CA, 128).transpose(2, 1, 0)
                                      .reshape(128, CA * CONV_A)),
        "wB_pp": np.ascontiguousarray(np.asarray(conv_b_w[0], f32).reshape(CONV_B, CA, 128).transpose(2, 1, 0)
                                      .reshape(128, CA * CONV_B)),
        "bB_pp": pp(conv_b_b[0], CA), "lng_pp": pp(ln_b_g[0], CA), "lnb_pp": pp(ln_b_b[0], CA),
        "fg_bc": np.ascontiguousarray(np.broadcast_to(np.asarray(final_g, f32)[None, :], (128, D))),
    }
    in_maps = []
    for core in range(n_cores):
        b, half = divmod(core, per_b)
        s0 = half * NTOK
        xs = np.asarray(x[b, s0:s0 + NTOK, :], f32)
        if s0 >= HALO:
            xh = np.asarray(x[b, s0 - HALO:s0, :], f32)
            mask = 1.0
        else:
            xh = np.zeros((HALO, D), f32)
            mask = 0.0
        m = dict(shared)
        m["x"] = np.ascontiguousarray(xs)
        m["xT"] = np.ascontiguousarray(xs.T)
        m["xTh"] = np.ascontiguousarray(xh.T)
        m["c_pp"] = pp(c[b], KC)
        m["hmask"] = np.full((128, 1), mask, f32)
        in_maps.append(m)
    return in_maps, NTOK


_CACHE = {}


def kernel(x, c, norm_g, w_ada, b_ada, w_in, conv_a_w, conv_b_w, conv_b_b, ln_b_g, ln_b_b, w_out, final_g,
           n_cores=8):
    x = np.asarray(x)
    B, SEQ, D = x.shape
    in_maps, NTOK = make_in_maps(x, np.asarray(c), np.asarray(norm_g), np.asarray(w_ada), np.asarray(b_ada),
                                 np.asarray(w_in), np.asarray(conv_a_w), np.asarray(conv_b_w),
                                 np.asarray(conv_b_b), np.asarray(ln_b_g), np.asarray(ln_b_b),
                                 np.asarray(w_out), np.asarray(final_g), n_cores)
    key = (D, NTOK)
    if key not in _CACHE:
        _CACHE[key] = build_program(D, NTOK)
    nc = _CACHE[key]
    res = run_bass_kernel_spmd(nc, in_maps, core_ids=list(range(n_cores)))
    per_b = n_cores // B
    out = np.empty((B, SEQ, D), np.float32)
    for core in range(n_cores):
        b, half = divmod(core, per_b)
        out[b, half * NTOK:(half + 1) * NTOK, :] = res.results[core]["out"]
    return out
```

```python
from contextlib import ExitStack
import numpy as np
import concourse.bass as bass
import concourse.mybir as mybir
from concourse.bass_utils import run_bass_kernel_spmd

F32 = mybir.dt.float32
BF16 = mybir.dt.bfloat16
ALU = mybir.AluOpType
AF = mybir.ActivationFunctionType
AX = mybir.AxisListType
EPS = 1e-6
CONV_A = 3
CONV_B = 31
HALO = 32
T = 512


class Tok:
    __slots__ = ("sem", "sid", "val")

    def __init__(self, sem, sid, val):
        self.sem = sem
        self.sid = sid
        self.val = val


class _Eng:
    def __init__(self, name, sem, sid):
        self.name = name
        self.sem = sem
        self.sid = sid
        self.count = 0
        self.ops = []
        self.waited = {}


class Sched:
    ENGS = ("pe", "act", "dve", "pool", "sp")

    def __init__(self, nc, stack, n_dma_sems=(("pool", 8), ("sp", 8), ("act", 6))):
        self.nc = nc
        self.e = {}
        self._sid = 0
        for n in self.ENGS:
            sem = stack.enter_context(nc.semaphore("s_" + n))
            self.e[n] = _Eng(n, sem, self._next_sid())
        self.dma_sems = {}
        self._rr = {}
        for q, n in n_dma_sems:
            self.dma_sems[q] = []
            self._rr[q] = 0
            for i in range(n):
                sem = stack.enter_context(nc.semaphore("d_%s%d" % (q, i)))
                self.dma_sems[q].append([sem, self._next_sid(), 0, None])
        self.last_w = {}
        self.readers = {}
        self.out_toks = []

    def _next_sid(self):
        self._sid += 1
        return self._sid

    def _deps(self, reads, writes):
        deps = []
        for k in reads:
            t = self.last_w.get(k)
            if t is not None:
                deps.append(t)
        for k in writes:
            t = self.last_w.get(k)
            if t is not None:
                deps.append(t)
            deps.extend(self.readers.get(k, ()))
        return deps

    def _waits(self, E, deps):
        best = {}
        for t in deps:
            if t is None:
                continue
            if E.waited.get(t.sid, 0) >= t.val:
                continue
            if t.sid not in best or best[t.sid].val < t.val:
                best[t.sid] = t
        for sid, t in best.items():
            E.waited[sid] = t.val
        return list(best.values())

    def _record(self, tok, reads, writes):
        for k in writes:
            self.last_w[k] = tok
            self.readers[k] = []
        for k in reads:
            self.readers.setdefault(k, []).append(tok)

    def op(self, eng, fn, reads=(), writes=()):
        E = self.e[eng]
        waits = self._waits(E, self._deps(reads, writes))
        E.count += 1
        tok = Tok(E.sem, E.sid, E.count)
        E.ops.append((waits, fn, E.sem, 1))
        self._record(tok, reads, writes)
        return tok

    def dma(self, queue, fn, reads=(), writes=(), is_output=False):
        E = self.e[queue]
        sl = self.dma_sems[queue]
        slot = sl[self._rr[queue]]
        self._rr[queue] = (self._rr[queue] + 1) % len(sl)
        deps = self._deps(reads, writes)
        if slot[3] is not None:
            deps.append(slot[3])
        waits = self._waits(E, deps)
        slot[2] += 16
        tok = Tok(slot[0], slot[1], slot[2])
        slot[3] = tok
        E.ops.append((waits, fn, slot[0], 16))
        self._record(tok, reads, writes)
        if is_output:
            self.out_toks.append(tok)
        return tok

    def finish(self, eng="sp"):
        E = self.e[eng]
        waits = self._waits(E, self.out_toks)
        E.ops.append((waits, None, None, 0))

    def emit(self, block):
        def run(E, h):
            for waits, fn, sem, inc in E.ops:
                for t in waits:
                    h.wait_ge(t.sem, t.val)
                if fn is not None:
                    fn(h).then_inc(sem, inc)

        if self.e["pe"].ops:
            @block.tensor
            def _(h):
                run(self.e["pe"], h)
        if self.e["act"].ops:
            @block.scalar
            def _(h):
                run(self.e["act"], h)
        if self.e["dve"].ops:
            @block.vector
            def _(h):
                run(self.e["dve"], h)
        if self.e["pool"].ops:
            @block.gpsimd
            def _(h):
                run(self.e["pool"], h)
        if self.e["sp"].ops:
            @block.sync
            def _(h):
                run(self.e["sp"], h)


def build_program(D, NTOK):
    KC = D // 128
    WA = D // 2
    CA = WA // 128
    CB = CA
    DIN = 7 * WA
    NT = NTOK // T
    NQ = D // 512
    KH = KC // 2
    SEG_AB, SEG_AC, SEG_AX, SEG_AZ = 0, WA, 2 * WA, 3 * WA
    SEG_BV, SEG_BG, SEG_BZ = 4 * WA, 5 * WA, 6 * WA
    NBUF = 3
    NGEN = 7
    NBANK = 5

    nc = bass.Bass("TRN2", target_bir_lowering=False)

    def din(name, shape):
        return nc.dram_tensor(name, shape, F32, kind="ExternalInput")

    xT_t = din("xT", [D, NTOK])
    xTh_t = din("xTh", [D, HALO])
    x_t = din("x", [NTOK, D])
    c_t = din("c_pp", [128, KC])
    wada_t = din("w_ada", [D, 3 * D])
    bss_t = din("b_ss_pp", [128, 2 * KC])
    bgate_t = din("bgate_bc", [128, D])
    ng_t = din("ng_pp", [128, KC])
    win_t = din("w_in", [D, DIN])
    wout_t = din("w_out", [D, D])
    wA_t = din("wA_pp", [128, CA * CONV_A])
    wB_t = din("wB_pp", [128, CB * CONV_B])
    bB_t = din("bB_pp", [128, CB])
    lng_t = din("lng_pp", [128, CB])
    lnb_t = din("lnb_pp", [128, CB])
    fg_t = din("fg_bc", [128, D])
    hm_t = din("hmask", [128, 1])
    out_t = nc.dram_tensor("out", [NTOK, D], F32, kind="ExternalOutput")
    gate_d = nc.dram_tensor("gate_d", [128, D], F32)

    xT_v = xT_t.ap().rearrange("(kc p) t -> p kc t", p=128)
    xTh_v = xTh_t.ap().rearrange("(kc p) t -> p kc t", p=128)
    wada_v = wada_t.ap().rearrange("(kc p) e -> p kc e", p=128)
    win_v = win_t.ap().rearrange("(kc p) e -> p kc e", p=128)
    wout_v = wout_t.ap().rearrange("(ec p) d -> p ec d", p=128)
    x_ap = x_t.ap()
    out_ap = out_t.ap()
    gate_ap = gate_d.ap()

    with ExitStack() as st:
        S = Sched(nc, st)

        def sb(name, shape, dt=F32):
            return st.enter_context(nc.sbuf_tensor(name, shape, dt))

        slab = [sb("slab%d" % i, [128, KC, 256], BF16) for i in range(NBUF)]
        hx = sb("hx", [128, 2 * D], F32)
        h = hx.bitcast(BF16).reshape([128, KC, T])
        y = sb("y", [128, KC, T], BF16)
        big = sb("big", [128, 2 * D], F32)
        gf = sb("gf", [128, 2, 512], F32)
        xa = [sb("xa%d" % i, [128, 2, T], F32) for i in range(2)]
        gen_t = [sb("gen%d" % i, [128, T], F32) for i in range(NGEN)]
        cxb = [sb("cx%d" % i, [128, T + 2], F32) for i in range(2)]
        convo = [sb("convo%d" % i, [128, T], F32) for i in range(2)]
        ubuf = [sb("ub%d" % i, [128, T + 30], F32) for i in range(2)]
        acc1 = sb("acc1", [128, T], F32)
        vsum = sb("vsum", [128, T], F32)
        qsum = sb("qsum", [128, T], F32)
        accss = sb("accss", [128, T], F32)
        rstd1 = sb("rstd1", [128, T], F32)
        lnr = sb("lnr", [128, T], F32)
        lnm = sb("lnm", [128, T], F32)
        hh = sb("hh", [128, KC, HALO], BF16)
        cxc = sb("cxc", [128, CA, 2], F32)
        ucar = sb("ucar", [128, CB, 30], F32)
        ssq = sb("ssq", [128, 4, 8], F32)
        sst = sb("sst", [128, 4], F32)
        cpp = sb("cpp", [128, KC], F32)
        mod = sb("mod", [128, 2 * KC], F32)
        bss = sb("bss", [128, 2 * KC], F32)
        ngp = sb("ngp", [128, KC], F32)
        gs = sb("gs", [128, KC], F32)
        wA = sb("wA", [128, CA * CONV_A], F32)
        wB = sb("wB", [128, CB * CONV_B], F32)
        bB = sb("bB", [128, CB], F32)
        lng = sb("lng", [128, CB], F32)
        lnb = sb("lnb", [128, CB], F32)
        hm = sb("hm", [128, 1], F32)
        ones_f = sb("ones_f", [128, 128], F32)
        ones_b = sb("ones_b", [128, 128], BF16)
        xh_all = bass.AP(xa[0], 0, [[2 * T, 128], [HALO, KC], [1, HALO]])
        sqh_all = bass.AP(xa[1], 0, [[2 * T, 128], [HALO, KC], [1, HALO]])

        def xh_kc(kc):
            return bass.AP(xa[0], kc * HALO, [[2 * T, 128], [1, HALO]])

        psum = [st.enter_context(nc.psum_tensor("ps%d" % i, [128, 512], F32)) for i in range(8)]
        PS_MISC, PS_S1, PS_S2 = 5, 6, 7

        def vbuf(c):
            return big[:, c * T:(c + 1) * T]

        def xr(jj):
            return big[:, jj * D:(jj + 1) * D] if jj < 2 else hx[:, (jj - 2) * D:(jj - 1) * D]

        def xr_slice(jj, q):
            t_, o = (big, jj) if jj < 2 else (hx, jj - 2)
            return t_[:, o * D + q * 512: o * D + (q + 1) * 512]

        def xr_key(jj, q):
            return ("big", jj * (D // 512) + q) if jj < 2 else ("hx", jj - 2, q)

        def xr_keys(jj):
            return [xr_key(jj, q) for q in range(D // 512)]

        def h_keys(kc):
            ks = [("hx", (kc * 256) // D, ((kc * 256) % D) // 512)]
            if kc == KC - 1:
                ks.append("hrd")
            return ks


        state = {"gen": 0, "bank": 0, "slab_issued": 0, "slab_used": 0}

        def gen():
            i = state["gen"]
            state["gen"] = (i + 1) % NGEN
            return gen_t[i], ("gen", i)

        def next_bank():
            b = state["bank"]
            state["bank"] = (b + 1) % NBANK
            return b

        def act(out, in_, func, reads, writes, **kw):
            return S.op("act", lambda e: e.activation(out=out, in_=in_, func=func, **kw), reads, writes)

        def tt(out, in0, in1, op, reads, writes):
            return S.op("dve", lambda e: e.tensor_tensor(out=out, in0=in0, in1=in1, op=op), reads, writes)

        def ts(out, in0, s1, s2, op0, op1, reads, writes):
            if s2 is None:
                return S.op("dve", lambda e: e.tensor_scalar(out=out, in0=in0, scalar1=s1, scalar2=None,
                                                             op0=op0), reads, writes)
            return S.op("dve", lambda e: e.tensor_scalar(out=out, in0=in0, scalar1=s1, scalar2=s2,
                                                         op0=op0, op1=op1), reads, writes)

        def stt(out, in0, scalar, in1, op0, op1, reads, writes):
            return S.op("dve", lambda e: e.scalar_tensor_tensor(out=out, in0=in0, scalar=scalar, in1=in1,
                                                                op0=op0, op1=op1), reads, writes)

        def cp(out, in_, reads, writes):
            return S.op("dve", lambda e: e.tensor_copy(out=out, in_=in_), reads, writes)

        def mm(out, pairs, reads, writes, start=True, stop=True):
            def fn(e):
                ins = None
                n = len(pairs)
                for i, (l, r) in enumerate(pairs):
                    ins = e.matmul(out, lhsT=l, rhs=r, start=(start and i == 0), stop=(stop and i == n - 1))
                return ins
            return S.op("pe", fn, reads, writes)

        def load(queue, out, in_, writes, reads=()):
            return S.dma(queue, lambda e: e.dma_start(out=out, in_=in_), reads=reads, writes=writes)

        def rsqrt_inplace(ap, key):
            act(ap, ap, AF.Sqrt, [], [key])
            S.op("dve", lambda e: e.reciprocal(out=ap, in_=ap), [], [key])

        NSLAB_T = 3 * CA + CB // 2 + 2 * NQ
        wcache = nc.dram_tensor("wcache", [NSLAB_T, 128, KC, 256], BF16)
        wc_ap = wcache.ap()
        slabs = []
        for s in range(KC):
            slabs.append(("w", [(0, 256, wada_v, s * 256)], None, "plain"))
        for s in range(D // 256):
            slabs.append(("w", [(0, 256, wada_v, 2 * D + s * 256)], None, "plain"))
        for i in range(NT):
            cid = 0

            def wmode(cid, i=i):
                fill_tile = 0 if (cid % 3 == 0 or NT == 1) else 1
                return "plain" if i < fill_tile else ("fill" if i == fill_tile else "cached")
            for c in range(CA):
                slabs.append(("w", [(0, 128, win_v, SEG_AC + c * 128), (128, 128, win_v, SEG_AX + c * 128)], cid, wmode(cid)))
                slabs.append(("w", [(0, 128, win_v, SEG_AZ + c * 128), (128, 128, win_v, SEG_AB + c * 128)], cid + 1, wmode(cid + 1)))
                slabs.append(("w", [(0, 128, win_v, SEG_BV + c * 128), (128, 128, win_v, SEG_BG + c * 128)], cid + 2, wmode(cid + 2)))
                cid += 3
            for c2 in range(CB // 2):
                slabs.append(("w", [(0, 256, win_v, SEG_BZ + c2 * 256)], cid, wmode(cid)))
                cid += 1
            for q in range(NQ):
                for eh in range(2):
                    slabs.append(("o", (q, eh), cid + 2 * q + eh, "fill" if i == 0 else "cached"))

        def slab_o_view(b):
            return bass.AP(slab[b], 0, [[KC * 256, 128], [512, KH], [1, 512]])

        BOTH = lambda b: [("slab", b, 0), ("slab", b, 1)]

        def issue_slab(n):
            b = n % NBUF
            kind, payload, cid, mode = slabs[n]
            if mode == "cached":
                S.dma("pool", lambda e, b=b, cid=cid: e.dma_start(out=slab[b][:], in_=wc_ap[cid]),
                      reads=[("wc", cid)], writes=BOTH(b))
                return
            if kind == "w":
                for (off, ncol, src_v, col) in payload:
                    keys = [("slab", b, off // 128)] if ncol == 128 else BOTH(b)
                    S.dma("pool", lambda e, b=b, off=off, ncol=ncol, src_v=src_v, col=col:
                          e.dma_start(out=slab[b][:, :, off:off + ncol], in_=src_v[:, :, col:col + ncol]),
                          writes=keys)
            else:
                q, eh = payload
                S.dma("pool", lambda e, b=b, q=q, eh=eh:
                      e.dma_start(out=slab_o_view(b), in_=wout_v[:, eh * KH:(eh + 1) * KH, q * 512:(q + 1) * 512]),
                      writes=BOTH(b))

        def writeback_slab(n):
            b = n % NBUF
            kind, payload, cid, mode = slabs[n]
            if mode == "fill":
                S.dma("pool", lambda e, b=b, cid=cid: e.dma_start(out=wc_ap[cid], in_=slab[b][:]),
                      reads=BOTH(b), writes=[("wc", cid)])

        def next_slab():
            n = state["slab_used"]
            while state["slab_issued"] < min(len(slabs), n + NBUF):
                k = state["slab_issued"]
                issue_slab(k)
                if k >= 1:
                    writeback_slab(k - 1)
                state["slab_issued"] += 1
            state["slab_used"] = n + 1
            return n % NBUF

        S.op("dve", lambda e: e.memset(ones_f[:], 1.0), [], ["ones_f"])
        S.op("dve", lambda e: e.memset(ones_b[:], 1.0), [], ["ones_b"])
        for (dst, src, key) in ((cpp, c_t, "cpp"), (bss, bss_t, "bss"), (ngp, ng_t, "ngp"), (wA, wA_t, "wA"),
                                (wB, wB_t, "wB"), (bB, bB_t, "bB"), (lng, lng_t, "lng"), (lnb, lnb_t, "lnb"),
                                (hm, hm_t, "hm")):
            load("sp", dst[:], src.ap(), [key])
        ts(wB[:], wB[:], 0.5, None, ALU.mult, None, [], ["wB"])

        cact = sb("cact", [128, KC], F32)
        cact_b = sb("cact_b", [128, KC], BF16)
        shift = mod

        def prologue():
            s1g0 = stage1_p1(0, PS_S1)
            stage1_halo_p1(PS_S2)
            _prologue_body(s1g0)
            pump(s1g0, 10 ** 6)

        def _prologue_body(s1g0):
          if True:
            act(cact[:], cpp[:], AF.Silu, ["cpp"], ["cact"])
            cp(cact_b[:], cact[:], ["cact"], ["cact_b"])
            for kc in range(KC):
                ts(y[:, kc, 0:128], ones_b[:], cact[:, kc:kc + 1], None, ALU.mult, None,
                   ["ones_b", "cact"], [("y", kc)])
            for s in range(KC):
                b = next_slab()
                for jj in range(2):
                    j = 2 * s + jj
                    pairs = [(slab[b][:, kc, jj * 128:(jj + 1) * 128], cact_b[:, kc:kc + 1]) for kc in range(KC)]
                    mm(psum[PS_MISC][:, j:j + 1], pairs, [("slab", b, 0), ("slab", b, 1), "cact_b"], [("ps", PS_MISC)])
                pump(s1g0, 1)
            tt(mod[:], psum[PS_MISC][:, 0:2 * KC], bss[:], ALU.add, ["bss"], ["mod", ("ps", PS_MISC)])
            stt(gs[:], mod[:, KC:2 * KC], 1.0, ngp[:], ALU.add, ALU.mult, ["mod", "ngp"], ["gs"])
            for s in range(D // 256):
                b = next_slab()
                bank = next_bank()
                pairs = [(y[:, kc, 0:128], slab[b][:, kc, :]) for kc in range(KC)]
                mm(psum[bank][:, 0:256], pairs, [("slab", b, 0), ("slab", b, 1)] + [("y", kc) for kc in range(KC)],
                   [("ps", bank)])
                g, gk = gen()
                bg, bgk = gen()
                load("sp", bg[:, 0:256], bgate_t.ap()[:, s * 256:(s + 1) * 256], [bgk])
                tt(g[:, 0:256], psum[bank][:, 0:256], bg[:, 0:256], ALU.add, [bgk], [gk, ("ps", bank)])
                S.dma("sp", lambda e, g=g, s=s: e.dma_start(out=gate_ap[:, s * 256:(s + 1) * 256], in_=g[:, 0:256]),
                      reads=[gk], writes=["gate_d"])

        def stage1_p1(i, bank=None):
            bank = PS_MISC if bank is None else bank
            t0 = i * T
            for g2 in range(KC // 2):
                xi = g2 % 2
                load("sp", xa[xi][:], xT_v[:, 2 * g2:2 * g2 + 2, t0:t0 + T], [("xa", xi)])
                for s in range(2):
                    kc = 2 * g2 + s
                    if kc == 0:
                        act(accss[:], xa[xi][:, s, :], AF.Square, [("xa", xi)], ["accss"])
                    else:
                        sq, sqk = gen()
                        act(sq[:], xa[xi][:, s, :], AF.Square, [("xa", xi)], [sqk])
                        tt(accss[:], accss[:], sq[:], ALU.add, [sqk], ["accss"])
                yield
            mm(psum[bank][:, 0:T], [(ones_f[:], accss[:])], ["ones_f", "accss"], [("ps", bank)])
            ts(rstd1[:], psum[bank][:, 0:T], 1.0 / D, EPS, ALU.mult, ALU.add, [], ["rstd1", ("ps", bank)])
            rsqrt_inplace(rstd1[:], "rstd1")
            yield

        def stage1_p2(i):
            t0 = i * T
            for kc in range(KC):
                g, gk = gen()
                load("sp", g[:], xT_v[:, kc, t0:t0 + T], [gk])
                stt(g[:], g[:], gs[:, kc:kc + 1], rstd1[:], ALU.mult, ALU.mult, ["gs", "rstd1"], [gk])
                act(h[:, kc, :], g[:], AF.Identity, [gk, "mod"], h_keys(kc), bias=shift[:, kc:kc + 1])

        def pump(g, n):
            for _ in range(n):
                if g is None:
                    return
                try:
                    next(g)
                except StopIteration:
                    return

        def stage1_halo_p1(bank):
            load("sp", xh_all, xTh_v, [("xa", 0)])
            act(sqh_all, xh_all, AF.Square, [("xa", 0)], [("xa", 1)])
            S.op("dve", lambda e: e.tensor_reduce(out=accss[:, 0:HALO],
                                                  in_=bass.AP(xa[1], 0, [[2 * T, 128], [1, HALO], [HALO, KC]]),
                                                  axis=AX.X, op=ALU.add), [("xa", 1)], ["accss"])
            mm(psum[bank][:, 0:HALO], [(ones_f[:], accss[:, 0:HALO])], ["ones_f", "accss"], [("ps", bank)])
            ts(lnr[:, 0:HALO], psum[bank][:, 0:HALO], 1.0 / D, EPS, ALU.mult, ALU.add, [], ["lnr", ("ps", bank)])
            rsqrt_inplace(lnr[:, 0:HALO], "lnr")

        def stage1_halo_p2():
            load("sp", xh_all, xTh_v, [("xa", 0)])
            for kc in range(KC):
                tmp, tk = gen()
                stt(tmp[:, 0:HALO], xh_kc(kc), gs[:, kc:kc + 1], lnr[:, 0:HALO], ALU.mult, ALU.mult,
                    [("xa", 0), "gs", "lnr"], [tk])
                act(hh[:, kc, :], tmp[:, 0:HALO], AF.Identity, [tk, "mod"], ["hh"], bias=shift[:, kc:kc + 1])

        def proj_group(b, piece, halo):
            bank = next_bank()
            pairs = [(slab[b][:, kc, piece * 128:(piece + 1) * 128], h[:, kc, :]) for kc in range(KC)]
            mm(psum[bank][:, 0:T], pairs, [("slab", b, piece), "hrd"], [("ps", bank)])
            hb = None
            if halo:
                hb = next_bank()
                pairs = [(slab[b][:, kc, piece * 128:(piece + 1) * 128], hh[:, kc, :]) for kc in range(KC)]
                mm(psum[hb][:, 0:HALO], pairs, [("slab", b, piece), "hh"], [("ps", hb)])
            return bank, hb

        def mixA_1(i, c, b_ac, h_ac, b_ax, h_ax):
            if i == 0:
                a, ak = gen()
                act(a[:, 0:HALO], psum[h_ac][:, 0:HALO], AF.Copy, [], [ak, ("ps", h_ac)])
                t_, tk = gen()
                tt(t_[:, 0:HALO], psum[h_ax][:, 0:HALO], a[:, 0:HALO], ALU.mult, [ak], [tk, ("ps", h_ax)])
                ts(cxc[:, c, :], t_[:, HALO - 2:HALO], hm[:, 0:1], None, ALU.mult, None, [tk, "hm"], [("cxc", c)])
            a, ak = gen()
            act(a[:], psum[b_ac][:, 0:T], AF.Copy, [], [ak, ("ps", b_ac)])
            cx = cxb[c % 2]
            ck = ("cx", c % 2)
            cp(cx[:, 0:2], cxc[:, c, :], [("cxc", c)], [ck])
            tt(cx[:, 2:T + 2], psum[b_ax][:, 0:T], a[:], ALU.mult, [ak], [ck, ("ps", b_ax)])
            cp(cxc[:, c, :], cx[:, T:T + 2], [ck], [("cxc", c)])
            co = convo[c % 2]
            cok = ("convo", c % 2)
            t_, tk = gen()
            w = lambda k: wA[:, c * CONV_A + k:c * CONV_A + k + 1]
            ts(co[:], cx[:, 0:T], w(0), None, ALU.mult, None, [ck, "wA"], [cok])
            stt(t_[:], cx[:, 1:T + 1], w(1), co[:], ALU.mult, ALU.add, [ck, "wA", cok], [tk])
            stt(co[:], cx[:, 2:T + 2], w(2), t_[:], ALU.mult, ALU.add, [ck, "wA", tk], [cok])

        def mixA_2(i, c, b_az, b_ab):
            sz, szk = gen()
            act(sz[:], psum[b_az][:, 0:T], AF.Silu, [], [szk, ("ps", b_az)])
            t4, t4k = gen()
            tt(t4[:], psum[b_ab][:, 0:T], convo[c % 2][:], ALU.mult, [("convo", c % 2)], [t4k, ("ps", b_ab)])
            tt(y[:, c, :], t4[:], sz[:], ALU.mult, [t4k, szk], [("y", c)])

        def mixB_1(i, c, b_bv, h_bv, b_bg, h_bg):
            if i == 0:
                th, thk = gen()
                act(th[:, 0:HALO], psum[h_bg][:, 0:HALO], AF.Tanh, [], [thk, ("ps", h_bg)], scale=0.5)
                u_, uk = gen()
                stt(u_[:, 0:HALO], th[:, 0:HALO], 1.0, psum[h_bv][:, 0:HALO], ALU.add, ALU.mult,
                    [thk], [uk, ("ps", h_bv)])
                ts(ucar[:, c, :], u_[:, HALO - 30:HALO], hm[:, 0:1], None, ALU.mult, None, [uk, "hm"], [("ucar", c)])
            th, thk = gen()
            act(th[:], psum[b_bg][:, 0:T], AF.Tanh, [], [thk, ("ps", b_bg)], scale=0.5)
            ub = ubuf[c % 2]
            ubk = ("ub", c % 2)
            cp(ub[:, 0:30], ucar[:, c, :], [("ucar", c)], [ubk])
            stt(ub[:, 30:T + 30], th[:], 1.0, psum[b_bv][:, 0:T], ALU.add, ALU.mult, [thk], [ubk, ("ps", b_bv)])
            cp(ucar[:, c, :], ub[:, T:T + 30], [ubk], [("ucar", c)])
            vk = ("big", c)
            bufs = [(vbuf(c), vk), (acc1[:], "acc1")]
            w = lambda k: wB[:, c * CONV_B + k:c * CONV_B + k + 1]
            ts(vbuf(c), ub[:, 0:T], w(0), bB[:, c:c + 1], ALU.mult, ALU.add, [ubk, "wB", "bB"], [vk])
            for k in range(1, CONV_B):
                o, ok = bufs[k % 2]
                p, pk = bufs[(k - 1) % 2]
                stt(o, ub[:, k:k + T], w(k), p, ALU.mult, ALU.add, [ubk, "wB", pk], [ok])

        def mixB_stats(c):
            if c == 0:
                cp(vsum[:], vbuf(c), [("big", c)], ["vsum"])
                act(qsum[:], vbuf(c), AF.Square, [("big", c)], ["qsum"])
            else:
                sq, sqk = gen()
                act(sq[:], vbuf(c), AF.Square, [("big", c)], [sqk])
                tt(vsum[:], vsum[:], vbuf(c), ALU.add, [("big", c)], ["vsum"])
                tt(qsum[:], qsum[:], sq[:], ALU.add, [sqk], ["qsum"])
            if c == CB - 1:
                mm(psum[PS_S1][:, 0:T], [(ones_f[:], vsum[:])], ["ones_f", "vsum"], [("ps", PS_S1)])
                mm(psum[PS_S2][:, 0:T], [(ones_f[:], qsum[:])], ["ones_f", "qsum"], [("ps", PS_S2)])

        def ln_finalize():
            mean, mk = gen()
            act(mean[:], psum[PS_S1][:, 0:T], AF.Copy, [], [mk, ("ps", PS_S1)], scale=1.0 / WA)
            msq, qk = gen()
            tt(msq[:], mean[:], mean[:], ALU.mult, [mk], [qk])
            stt(lnr[:], psum[PS_S2][:, 0:T], 1.0 / WA, msq[:], ALU.mult, ALU.subtract, [qk], ["lnr", ("ps", PS_S2)])
            ts(lnr[:], lnr[:], EPS, None, ALU.add, None, [], ["lnr"])
            rsqrt_inplace(lnr[:], "lnr")
            stt(lnm[:], mean[:], -1.0, lnr[:], ALU.mult, ALU.mult, [mk, "lnr"], ["lnm"])

        def mixB_2(c, b_bz):
            t1, k1 = gen()
            tt(t1[:], vbuf(c), lnr[:], ALU.mult, [("big", c), "lnr"], [k1])
            t2, k2 = gen()
            tt(t2[:], t1[:], lnm[:], ALU.add, [k1, "lnm"], [k2])
            s1, sk = gen()
            act(s1[:], t2[:], AF.Silu, [k2, "lng", "lnb"], [sk], scale=lng[:, c:c + 1], bias=lnb[:, c:c + 1])
            sz, szk = gen()
            act(sz[:], psum[b_bz][:, 0:T], AF.Silu, [], [szk, ("ps", b_bz)])
            tt(y[:, CA + c, :], s1[:], sz[:], ALU.mult, [sk, szk], [("y", CA + c)])

        def outproj_tile(i, s1g, npump):
            NB = T // 128
            order = [2, 3, 0, 1]
            rows = [i * T + jj * 128 for jj in range(NB)]
            for jj in (0, 1):
                load("sp", xr(jj), x_ap[rows[jj]:rows[jj] + 128, :], xr_keys(jj))
            for q in range(NQ):
                load("sp", gf[:, 0, :], gate_ap[:, q * 512:(q + 1) * 512], ["gf0"], reads=["gate_d"])
                load("sp", gf[:, 1, :], fg_t.ap()[:, q * 512:(q + 1) * 512], ["gf1"])
                if q == 0:
                    for jj in (2, 3):
                        load("sp", xr(jj), x_ap[rows[jj]:rows[jj] + 128, :], xr_keys(jj) + ["hrd"])
                banks = {jj: next_bank() for jj in order}
                for eh in range(2):
                    b = next_slab()
                    for jj in order:
                        tcol = jj * 128
                        pairs = [(y[:, eh * KH + e, tcol:tcol + 128],
                                  bass.AP(slab[b], e * 512, [[KC * 256, 128], [1, 512]])) for e in range(KH)]
                        mm(psum[banks[jj]][:, 0:512], pairs,
                           [("slab", b, 0), ("slab", b, 1)] + [("y", eh * KH + e) for e in range(KH)],
                           [("ps", banks[jj])],
                           start=(eh == 0), stop=(eh == 1))
                for jj in order:
                    kq = xr_key(jj, q)
                    xs = xr_slice(jj, q)
                    tmp, tk = gen()
                    tt(tmp[:], psum[banks[jj]][:, 0:512], gf[:, 0, :], ALU.mult, ["gf0"], [tk, ("ps", banks[jj])])
                    tt(xs, tmp[:], xs, ALU.add, [tk], [kq])
                    sq, sqk = gen()
                    act(sq[:], xs, AF.Square, [kq], [sqk, ("ssq", jj)], accum_out=ssq[:, jj, q:q + 1])
                    tt(xs, xs, gf[:, 1, :], ALU.mult, ["gf1"], [kq])
                pump(s1g, npump)
            pump(s1g, 10 ** 6)
            def rstd_pair(j0):
                key = "sst%d" % j0
                for jj in (j0, j0 + 1):
                    S.op("dve", lambda e, jj=jj: e.tensor_reduce(out=sst[:, jj:jj + 1], in_=ssq[:, jj, 0:NQ],
                                                                axis=AX.X, op=ALU.add), [("ssq", jj)], [key])
                ts(sst[:, j0:j0 + 2], sst[:, j0:j0 + 2], 1.0 / D, EPS, ALU.mult, ALU.add, [], [key])
                rsqrt_inplace(sst[:, j0:j0 + 2], key)

            def fin(jj):
                if jj >= 2 and i + 1 < NT:
                    for q in range(NQ):
                        xs = xr_slice(jj, q)
                        ts(xs, xs, sst[:, jj:jj + 1], None, ALU.mult, None, ["sst2"], [xr_key(jj, q)])
                        S.dma("sp", lambda e, jj=jj, q=q, xs=xs:
                              e.dma_start(out=out_ap[rows[jj]:rows[jj] + 128, q * 512:(q + 1) * 512], in_=xs),
                              reads=[xr_key(jj, q)], is_output=True)
                    return
                act(xr(jj), xr(jj), AF.Copy, ["sst%d" % (2 * (jj // 2))], xr_keys(jj), scale=sst[:, jj:jj + 1])
                S.dma("act", lambda e, jj=jj: e.dma_start(out=out_ap[rows[jj]:rows[jj] + 128, :], in_=xr(jj)),
                      reads=xr_keys(jj), is_output=True)

            rstd_pair(2)
            fin(2)
            fin(3)
            if i + 1 < NT:
                stage1_p2(i + 1)
            rstd_pair(0)
            fin(0)
            fin(1)

        prologue()
        stage1_halo_p2()
        stage1_p2(0)
        for i in range(NT):
            halo = (i == 0)
            pend_stats = None
            for c in range(CA):
                b = next_slab()
                b_ac, h_ac = proj_group(b, 0, halo)
                b_ax, h_ax = proj_group(b, 1, halo)
                mixA_1(i, c, b_ac, h_ac, b_ax, h_ax)
                b = next_slab()
                b_az, _ = proj_group(b, 0, False)
                b_ab, _ = proj_group(b, 1, False)
                mixA_2(i, c, b_az, b_ab)
                b = next_slab()
                b_bv, h_bv = proj_group(b, 0, halo)
                b_bg, h_bg = proj_group(b, 1, halo)
                if pend_stats is not None:
                    mixB_stats(pend_stats)
                mixB_1(i, c, b_bv, h_bv, b_bg, h_bg)
                pend_stats = c
            bz_banks = {}
            for c2 in range(CB // 2):
                b = next_slab()
                for s in range(2):
                    bz_banks[2 * c2 + s], _ = proj_group(b, s, False)
                if c2 == 0:
                    mixB_stats(pend_stats)
                    ln_finalize()
                for s in range(2):
                    mixB_2(2 * c2 + s, bz_banks[2 * c2 + s])
            s1g = stage1_p1(i + 1) if i + 1 < NT else None
            outproj_tile(i, s1g, -(-(KC // 2 + 1) // NQ))
        S.finish("act")
        with nc.Block() as block:
            S.emit(block)
    return nc


def make_in_maps(x, c, norm_g, w_ada, b_ada, w_in, conv_a_w, conv_b_w, conv_b_b, ln_b_g, ln_b_b, w_out,
                 final_g, n_cores):
    B, SEQ, D = x.shape
    per_b = n_cores // B
    NTOK = SEQ // per_b
    KC = D // 128
    WA = D // 2
    CA = WA // 128
    f32 = np.float32

    def pp(v, n):
        return np.ascontiguousarray(np.asarray(v, f32).reshape(n, 128).T)

    w_ada0 = np.ascontiguousarray(np.asarray(w_ada[0], f32))
    w_in0 = np.ascontiguousarray(np.asarray(w_in[0], f32))
    w_out0 = np.ascontiguousarray(np.asarray(w_out[0], f32))
    b_ada0 = np.asarray(b_ada[0], f32)
    shared = {
        "w_ada": w_ada0, "w_in": w_in0, "w_out": w_out0,
        "b_ss_pp": pp(b_ada0[0:2 * D], 2 * KC),
        "bgate_bc": np.ascontiguousarray(np.broadcast_to(b_ada0[2 * D:3 * D][None, :], (128, D))),
        "ng_pp": pp(norm_g[0], KC),
        "wA_pp": np.ascontiguousarray(np.asarray(conv_a_w[0], f32).reshape(CONV_A, CA, 128).transpose(2, 1, 0)
                                      .reshape(128, CA * CONV_A)),
        "wB_pp": np.ascontiguousarray(np.asarray(conv_b_w[0], f32).reshape(CONV_B, CA, 128).transpose(2, 1, 0)
                                      .reshape(128, CA * CONV_B)),
        "bB_pp": pp(conv_b_b[0], CA), "lng_pp": pp(ln_b_g[0], CA), "lnb_pp": pp(ln_b_b[0], CA),
        "fg_bc": np.ascontiguousarray(np.broadcast_to(np.asarray(final_g, f32)[None, :], (128, D))),
    }
    in_maps = []
    for core in range(n_cores):
        b, half = divmod(core, per_b)
        s0 = half * NTOK
        xs = np.asarray(x[b, s0:s0 + NTOK, :], f32)
        if s0 >= HALO:
            xh = np.asarray(x[b, s0 - HALO:s0, :], f32)
            mask = 1.0
        else:
            xh = np.zeros((HALO, D), f32)
            mask = 0.0
        m = dict(shared)
        m["x"] = np.ascontiguousarray(xs)
        m["xT"] = np.ascontiguousarray(xs.T)
        m["xTh"] = np.ascontiguousarray(xh.T)
        m["c_pp"] = pp(c[b], KC)
        m["hmask"] = np.full((128, 1), mask, f32)
        in_maps.append(m)
    return in_maps, NTOK


_CACHE = {}


def kernel(x, c, norm_g, w_ada, b_ada, w_in, conv_a_w, conv_b_w, conv_b_b, ln_b_g, ln_b_b, w_out, final_g,
           n_cores=8):
    x = np.asarray(x)
    B, SEQ, D = x.shape
    in_maps, NTOK = make_in_maps(x, np.asarray(c), np.asarray(norm_g), np.asarray(w_ada), np.asarray(b_ada),
                                 np.asarray(w_in), np.asarray(conv_a_w), np.asarray(conv_b_w),
                                 np.asarray(conv_b_b), np.asarray(ln_b_g), np.asarray(ln_b_b),
                                 np.asarray(w_out), np.asarray(final_g), n_cores)
    key = (D, NTOK)
    if key not in _CACHE:
        _CACHE[key] = build_program(D, NTOK)
    nc = _CACHE[key]
    res = run_bass_kernel_spmd(nc, in_maps, core_ids=list(range(n_cores)))
    per_b = n_cores // B
    out = np.empty((B, SEQ, D), np.float32)
    for core in range(n_cores):
        b, half = divmod(core, per_b)
        out[b, half * NTOK:(half + 1) * NTOK, :] = res.results[core]["out"]
    return out
```

```python
from contextlib import ExitStack
import numpy as np
import concourse.bass as bass
import concourse.mybir as mybir
from concourse.bass_utils import run_bass_kernel_spmd

F32 = mybir.dt.float32
BF16 = mybir.dt.bfloat16
ALU = mybir.AluOpType
AF = mybir.ActivationFunctionType
AX = mybir.AxisListType
EPS = 1e-6
CONV_A = 3
CONV_B = 31
HALO = 32
T = 512


class Tok:
    __slots__ = ("sem", "sid", "val")

    def __init__(self, sem, sid, val):
        self.sem = sem
        self.sid = sid
        self.val = val


class _Eng:
    def __init__(self, name, sem, sid):
        self.name = name
        self.sem = sem
        self.sid = sid
        self.count = 0
        self.ops = []
        self.waited = {}


class Sched:
    ENGS = ("pe", "act", "dve", "pool", "sp")

    def __init__(self, nc, stack, n_dma_sems=(("pool", 8), ("sp", 8), ("act", 6))):
        self.nc = nc
        self.e = {}
        self._sid = 0
        for n in self.ENGS:
            sem = stack.enter_context(nc.semaphore("s_" + n))
            self.e[n] = _Eng(n, sem, self._next_sid())
        self.dma_sems = {}
        self._rr = {}
        for q, n in n_dma_sems:
            self.dma_sems[q] = []
            self._rr[q] = 0
            for i in range(n):
                sem = stack.enter_context(nc.semaphore("d_%s%d" % (q, i)))
                self.dma_sems[q].append([sem, self._next_sid(), 0, None])
        self.last_w = {}
        self.readers = {}
        self.out_toks = []

    def _next_sid(self):
        self._sid += 1
        return self._sid

    def _deps(self, reads, writes):
        deps = []
        for k in reads:
            t = self.last_w.get(k)
            if t is not None:
                deps.append(t)
        for k in writes:
            t = self.last_w.get(k)
            if t is not None:
                deps.append(t)
            deps.extend(self.readers.get(k, ()))
        return deps

    def _waits(self, E, deps):
        best = {}
        for t in deps:
            if t is None:
                continue
            if E.waited.get(t.sid, 0) >= t.val:
                continue
            if t.sid not in best or best[t.sid].val < t.val:
                best[t.sid] = t
        for sid, t in best.items():
            E.waited[sid] = t.val
        return list(best.values())

    def _record(self, tok, reads, writes):
        for k in writes:
            self.last_w[k] = tok
            self.readers[k] = []
        for k in reads:
            self.readers.setdefault(k, []).append(tok)

    def op(self, eng, fn, reads=(), writes=()):
        E = self.e[eng]
        waits = self._waits(E, self._deps(reads, writes))
        E.count += 1
        tok = Tok(E.sem, E.sid, E.count)
        E.ops.append((waits, fn, E.sem, 1))
        self._record(tok, reads, writes)
        return tok

    def dma(self, queue, fn, reads=(), writes=(), is_output=False):
        E = self.e[queue]
        sl = self.dma_sems[queue]
        slot = sl[self._rr[queue]]
        self._rr[queue] = (self._rr[queue] + 1) % len(sl)
        deps = self._deps(reads, writes)
        if slot[3] is not None:
            deps.append(slot[3])
        waits = self._waits(E, deps)
        slot[2] += 16
        tok = Tok(slot[0], slot[1], slot[2])
        slot[3] = tok
        E.ops.append((waits, fn, slot[0], 16))
        self._record(tok, reads, writes)
        if is_output:
            self.out_toks.append(tok)
        return tok

    def finish(self, eng="sp"):
        E = self.e[eng]
        waits = self._waits(E, self.out_toks)
        E.ops.append((waits, None, None, 0))

    def emit(self, block):
        def run(E, h):
            for waits, fn, sem, inc in E.ops:
                for t in waits:
                    h.wait_ge(t.sem, t.val)
                if fn is not None:
                    fn(h).then_inc(sem, inc)

        if self.e["pe"].ops:
            @block.tensor
            def _(h):
                run(self.e["pe"], h)
        if self.e["act"].ops:
            @block.scalar
            def _(h):
                run(self.e["act"], h)
        if self.e["dve"].ops:
            @block.vector
            def _(h):
                run(self.e["dve"], h)
        if self.e["pool"].ops:
            @block.gpsimd
            def _(h):
                run(self.e["pool"], h)
        if self.e["sp"].ops:
            @block.sync
            def _(h):
                run(self.e["sp"], h)


def build_program(D, NTOK):
    KC = D // 128
    WA = D // 2
    CA = WA // 128
    CB = CA
    DIN = 7 * WA
    NT = NTOK // T
    NQ = D // 512
    KH = KC // 2
    SEG_AB, SEG_AC, SEG_AX, SEG_AZ = 0, WA, 2 * WA, 3 * WA
    SEG_BV, SEG_BG, SEG_BZ = 4 * WA, 5 * WA, 6 * WA
    NBUF = 3
    NGEN = 7
    NBANK = 5

    nc = bass.Bass("TRN2", target_bir_lowering=False)

    def din(name, shape):
        return nc.dram_tensor(name, shape, F32, kind="ExternalInput")

    xT_t = din("xT", [D, NTOK])
    xTh_t = din("xTh", [D, HALO])
    x_t = din("x", [NTOK, D])
    c_t = din("c_pp", [128, KC])
    wada_t = din("w_ada", [D, 3 * D])
    bss_t = din("b_ss_pp", [128, 2 * KC])
    bgate_t = din("bgate_bc", [128, D])
    ng_t = din("ng_pp", [128, KC])
    win_t = din("w_in", [D, DIN])
    wout_t = din("w_out", [D, D])
    wA_t = din("wA_pp", [128, CA * CONV_A])
    wB_t = din("wB_pp", [128, CB * CONV_B])
    bB_t = din("bB_pp", [128, CB])
    lng_t = din("lng_pp", [128, CB])
    lnb_t = din("lnb_pp", [128, CB])
    fg_t = din("fg_bc", [128, D])
    hm_t = din("hmask", [128, 1])
    out_t = nc.dram_tensor("out", [NTOK, D], F32, kind="ExternalOutput")
    gate_d = nc.dram_tensor("gate_d", [128, D], F32)

    xT_v = xT_t.ap().rearrange("(kc p) t -> p kc t", p=128)
    xTh_v = xTh_t.ap().rearrange("(kc p) t -> p kc t", p=128)
    wada_v = wada_t.ap().rearrange("(kc p) e -> p kc e", p=128)
    win_v = win_t.ap().rearrange("(kc p) e -> p kc e", p=128)
    wout_v = wout_t.ap().rearrange("(ec p) d -> p ec d", p=128)
    x_ap = x_t.ap()
    out_ap = out_t.ap()
    gate_ap = gate_d.ap()

    with ExitStack() as st:
        S = Sched(nc, st)

        def sb(name, shape, dt=F32):
            return st.enter_context(nc.sbuf_tensor(name, shape, dt))

        slab = [sb("slab%d" % i, [128, KC, 256], BF16) for i in range(NBUF)]
        hx = sb("hx", [128, 2 * D], F32)
        h = hx.bitcast(BF16).reshape([128, KC, T])
        y = sb("y", [128, KC, T], BF16)
        big = sb("big", [128, 2 * D], F32)
        gf = sb("gf", [128, 2, 512], F32)
        xa = [sb("xa%d" % i, [128, 2, T], F32) for i in range(2)]
        gen_t = [sb("gen%d" % i, [128, T], F32) for i in range(NGEN)]
        cxb = [sb("cx%d" % i, [128, T + 2], F32) for i in range(2)]
        convo = [sb("convo%d" % i, [128, T], F32) for i in range(2)]
        ubuf = [sb("ub%d" % i, [128, T + 30], F32) for i in range(2)]
        acc1 = sb("acc1", [128, T], F32)
        vsum = sb("vsum", [128, T], F32)
        qsum = sb("qsum", [128, T], F32)
        accss = sb("accss", [128, T], F32)
        rstd1 = sb("rstd1", [128, T], F32)
        lnr = sb("lnr", [128, T], F32)
        lnm = sb("lnm", [128, T], F32)
        hh = sb("hh", [128, KC, HALO], BF16)
        cxc = sb("cxc", [128, CA, 2], F32)
        ucar = sb("ucar", [128, CB, 30], F32)
        ssq = sb("ssq", [128, 4, 8], F32)
        sst = sb("sst", [128, 4], F32)
        cpp = sb("cpp", [128, KC], F32)
        mod = sb("mod", [128, 2 * KC], F32)
        bss = sb("bss", [128, 2 * KC], F32)
        ngp = sb("ngp", [128, KC], F32)
        gs = sb("gs", [128, KC], F32)
        wA = sb("wA", [128, CA * CONV_A], F32)
        wB = sb("wB", [128, CB * CONV_B], F32)
        bB = sb("bB", [128, CB], F32)
        lng = sb("lng", [128, CB], F32)
        lnb = sb("lnb", [128, CB], F32)
        hm = sb("hm", [128, 1], F32)
        ones_f = sb("ones_f", [128, 128], F32)
        ones_b = sb("ones_b", [128, 128], BF16)
        xh_all = bass.AP(xa[0], 0, [[2 * T, 128], [HALO, KC], [1, HALO]])
        sqh_all = bass.AP(xa[1], 0, [[2 * T, 128], [HALO, KC], [1, HALO]])

        def xh_kc(kc):
            return bass.AP(xa[0], kc * HALO, [[2 * T, 128], [1, HALO]])

        psum = [st.enter_context(nc.psum_tensor("ps%d" % i, [128, 512], F32)) for i in range(8)]
        PS_MISC, PS_S1, PS_S2 = 5, 6, 7

        def vbuf(c):
            return big[:, c * T:(c + 1) * T]

        def xr(jj):
            return big[:, jj * D:(jj + 1) * D] if jj < 2 else hx[:, (jj - 2) * D:(jj - 1) * D]

        def xr_slice(jj, q):
            t_, o = (big, jj) if jj < 2 else (hx, jj - 2)
            return t_[:, o * D + q * 512: o * D + (q + 1) * 512]

        def xr_key(jj, q):
            return ("big", jj * (D // 512) + q) if jj < 2 else ("hx", jj - 2, q)

        def xr_keys(jj):
            return [xr_key(jj, q) for q in range(D // 512)]

        def h_keys(kc):
            ks = [("hx", (kc * 256) // D, ((kc * 256) % D) // 512)]
            if kc == KC - 1:
                ks.append("hrd")
            return ks


        state = {"gen": 0, "bank": 0, "slab_issued": 0, "slab_used": 0}

        def gen():
            i = state["gen"]
            state["gen"] = (i + 1) % NGEN
            return gen_t[i], ("gen", i)

        def next_bank():
            b = state["bank"]
            state["bank"] = (b + 1) % NBANK
            return b

        def act(out, in_, func, reads, writes, **kw):
            return S.op("act", lambda e: e.activation(out=out, in_=in_, func=func, **kw), reads, writes)

        def tt(out, in0, in1, op, reads, writes):
            return S.op("dve", lambda e: e.tensor_tensor(out=out, in0=in0, in1=in1, op=op), reads, writes)

        def ts(out, in0, s1, s2, op0, op1, reads, writes):
            if s2 is None:
                return S.op("dve", lambda e: e.tensor_scalar(out=out, in0=in0, scalar1=s1, scalar2=None,
                                                             op0=op0), reads, writes)
            return S.op("dve", lambda e: e.tensor_scalar(out=out, in0=in0, scalar1=s1, scalar2=s2,
                                                         op0=op0, op1=op1), reads, writes)

        def stt(out, in0, scalar, in1, op0, op1, reads, writes):
            return S.op("dve", lambda e: e.scalar_tensor_tensor(out=out, in0=in0, scalar=scalar, in1=in1,
                                                                op0=op0, op1=op1), reads, writes)

        def cp(out, in_, reads, writes):
            return S.op("dve", lambda e: e.tensor_copy(out=out, in_=in_), reads, writes)

        def mm(out, pairs, reads, writes, start=True, stop=True):
            def fn(e):
                ins = None
                n = len(pairs)
                for i, (l, r) in enumerate(pairs):
                    ins = e.matmul(out, lhsT=l, rhs=r, start=(start and i == 0), stop=(stop and i == n - 1))
                return ins
            return S.op("pe", fn, reads, writes)

        def load(queue, out, in_, writes, reads=()):
            return S.dma(queue, lambda e: e.dma_start(out=out, in_=in_), reads=reads, writes=writes)

        def rsqrt_inplace(ap, key):
            act(ap, ap, AF.Sqrt, [], [key])
            S.op("dve", lambda e: e.reciprocal(out=ap, in_=ap), [], [key])

        NSLAB_T = 3 * CA + CB // 2 + 2 * NQ
        wcache = nc.dram_tensor("wcache", [NSLAB_T, 128, KC, 256], BF16)
        wc_ap = wcache.ap()
        slabs = []
        for cb in range(3 * D // 512):
            for kh in range(2):
                slabs.append(("a", (cb, kh), None, "plain"))
        for i in range(NT):
            cid = 0

            def wmode(cid, i=i):
                fill_tile = 0 if (cid % 3 == 0 or NT == 1) else 1
                return "plain" if i < fill_tile else ("fill" if i == fill_tile else "cached")
            for c in range(CA):
                slabs.append(("w", [(0, 128, win_v, SEG_AC + c * 128), (128, 128, win_v, SEG_AX + c * 128)], cid, wmode(cid)))
                slabs.append(("w", [(0, 128, win_v, SEG_AZ + c * 128), (128, 128, win_v, SEG_AB + c * 128)], cid + 1, wmode(cid + 1)))
                slabs.append(("w", [(0, 128, win_v, SEG_BV + c * 128), (128, 128, win_v, SEG_BG + c * 128)], cid + 2, wmode(cid + 2)))
                cid += 3
            for c2 in range(CB // 2):
                slabs.append(("w", [(0, 256, win_v, SEG_BZ + c2 * 256)], cid, wmode(cid)))
                cid += 1
            for q in range(NQ):
                for eh in range(2):
                    slabs.append(("o", (q, eh), cid + 2 * q + eh, "fill" if i == 0 else "cached"))

        def slab_o_view(b):
            return bass.AP(slab[b], 0, [[KC * 256, 128], [512, KH], [1, 512]])

        BOTH = lambda b: [("slab", b, 0), ("slab", b, 1)]

        def issue_slab(n):
            b = n % NBUF
            kind, payload, cid, mode = slabs[n]
            if mode == "cached":
                S.dma("pool", lambda e, b=b, cid=cid: e.dma_start(out=slab[b][:], in_=wc_ap[cid]),
                      reads=[("wc", cid)], writes=BOTH(b))
                return
            if kind == "w":
                for (off, ncol, src_v, col) in payload:
                    keys = [("slab", b, off // 128)] if ncol == 128 else BOTH(b)
                    S.dma("pool", lambda e, b=b, off=off, ncol=ncol, src_v=src_v, col=col:
                          e.dma_start(out=slab[b][:, :, off:off + ncol], in_=src_v[:, :, col:col + ncol]),
                          writes=keys)
            else:
                q, eh = payload
                src_v = wout_v if kind == "o" else wada_v
                S.dma("pool", lambda e, b=b, q=q, eh=eh, src_v=src_v:
                      e.dma_start(out=slab_o_view(b), in_=src_v[:, eh * KH:(eh + 1) * KH, q * 512:(q + 1) * 512]),
                      writes=BOTH(b))

        def writeback_slab(n):
            b = n % NBUF
            kind, payload, cid, mode = slabs[n]
            if mode == "fill":
                S.dma("pool", lambda e, b=b, cid=cid: e.dma_start(out=wc_ap[cid], in_=slab[b][:]),
                      reads=BOTH(b), writes=[("wc", cid)])

        def next_slab():
            n = state["slab_used"]
            while state["slab_issued"] < min(len(slabs), n + NBUF):
                k = state["slab_issued"]
                issue_slab(k)
                if k >= 1:
                    writeback_slab(k - 1)
                state["slab_issued"] += 1
            state["slab_used"] = n + 1
            return n % NBUF

        S.op("dve", lambda e: e.memset(ones_f[:], 1.0), [], ["ones_f"])
        S.op("dve", lambda e: e.memset(ones_b[:], 1.0), [], ["ones_b"])
        for (dst, src, key) in ((cpp, c_t, "cpp"), (bss, bss_t, "bss"), (ngp, ng_t, "ngp"), (wA, wA_t, "wA"),
                                (wB, wB_t, "wB"), (bB, bB_t, "bB"), (lng, lng_t, "lng"), (lnb, lnb_t, "lnb"),
                                (hm, hm_t, "hm")):
            load("sp", dst[:], src.ap(), [key])
        ts(wB[:], wB[:], 0.5, None, ALU.mult, None, [], ["wB"])

        cact = sb("cact", [128, KC], F32)
        cact_b = sb("cact_b", [128, KC], BF16)
        shift = mod

        def prologue():
            s1g0 = stage1_p1(0, PS_S1)
            stage1_halo_p1(PS_S2)
            _prologue_body(s1g0)
            pump(s1g0, 10 ** 6)

        def _prologue_body(s1g0):
          if True:
            act(cact[:], cpp[:], AF.Silu, ["cpp"], ["cact"])
            cp(cact_b[:], cact[:], ["cact"], ["cact_b"])
            for kc in range(KC):
                ts(y[:, kc, 0:128], ones_b[:], cact[:, kc:kc + 1], None, ALU.mult, None,
                   ["ones_b", "cact"], [("y", kc)])
            def slab_k(b, e, c0, n):
                return bass.AP(slab[b], e * 512 + c0, [[KC * 256, 128], [1, n]])

            for cb in range(2 * D // 512):
                for kh in range(2):
                    b = next_slab()
                    for jj in range(4):
                        j = 4 * cb + jj
                        pairs = [(slab_k(b, e, jj * 128, 128), cact_b[:, kh * KH + e:kh * KH + e + 1]) for e in range(KH)]
                        col = kh * 2 * KC + j
                        mm(psum[PS_MISC][:, col:col + 1], pairs, [("slab", b, 0), ("slab", b, 1), "cact_b"],
                           [("ps", PS_MISC)])
                    pump(s1g0, 1)
            tt(mod[:], psum[PS_MISC][:, 0:2 * KC], bss[:], ALU.add, ["bss"], ["mod", ("ps", PS_MISC)])
            tt(mod[:], psum[PS_MISC][:, 2 * KC:4 * KC], mod[:], ALU.add, [], ["mod", ("ps", PS_MISC)])
            stt(gs[:], mod[:, KC:2 * KC], 1.0, ngp[:], ALU.add, ALU.mult, ["mod", "ngp"], ["gs"])
            for q in range(NQ):
                bank = next_bank()
                for kh in range(2):
                    b = next_slab()
                    pairs = [(y[:, kh * KH + e, 0:128], slab_k(b, e, 0, 512)) for e in range(KH)]
                    mm(psum[bank][:, 0:512], pairs,
                       [("slab", b, 0), ("slab", b, 1)] + [("y", kh * KH + e) for e in range(KH)],
                       [("ps", bank)], start=(kh == 0), stop=(kh == 1))
                g, gk = gen()
                bg, bgk = gen()
                load("sp", bg[:], bgate_t.ap()[:, q * 512:(q + 1) * 512], [bgk])
                tt(g[:], psum[bank][:, 0:512], bg[:], ALU.add, [bgk], [gk, ("ps", bank)])
                S.dma("sp", lambda e, g=g, q=q: e.dma_start(out=gate_ap[:, q * 512:(q + 1) * 512], in_=g[:]),
                      reads=[gk], writes=["gate_d"])

        def stage1_p1(i, bank=None):
            bank = PS_MISC if bank is None else bank
            t0 = i * T
            for g2 in range(KC // 2):
                xi = g2 % 2
                load("sp", xa[xi][:], xT_v[:, 2 * g2:2 * g2 + 2, t0:t0 + T], [("xa", xi)])
                for s in range(2):
                    kc = 2 * g2 + s
                    if kc == 0:
                        act(accss[:], xa[xi][:, s, :], AF.Square, [("xa", xi)], ["accss"])
                    else:
                        sq, sqk = gen()
                        act(sq[:], xa[xi][:, s, :], AF.Square, [("xa", xi)], [sqk])
                        tt(accss[:], accss[:], sq[:], ALU.add, [sqk], ["accss"])
                yield
            mm(psum[bank][:, 0:T], [(ones_f[:], accss[:])], ["ones_f", "accss"], [("ps", bank)])
            ts(rstd1[:], psum[bank][:, 0:T], 1.0 / D, EPS, ALU.mult, ALU.add, [], ["rstd1", ("ps", bank)])
            rsqrt_inplace(rstd1[:], "rstd1")
            yield

        def stage1_p2_prefetch(i, n):
            t0 = i * T
            pre = []
            for kc in range(min(n, KC)):
                g, gk = gen()
                load("sp", g[:], xT_v[:, kc, t0:t0 + T], [gk])
                pre.append((g, gk))
            return pre

        def stage1_p2(i, pre=()):
            t0 = i * T
            for kc in range(KC):
                if kc < len(pre):
                    g, gk = pre[kc]
                else:
                    g, gk = gen()
                    load("sp", g[:], xT_v[:, kc, t0:t0 + T], [gk])
                stt(g[:], g[:], gs[:, kc:kc + 1], rstd1[:], ALU.mult, ALU.mult, ["gs", "rstd1"], [gk])
                act(h[:, kc, :], g[:], AF.Identity, [gk, "mod"], h_keys(kc), bias=shift[:, kc:kc + 1])

        def pump(g, n):
            for _ in range(n):
                if g is None:
                    return
                try:
                    next(g)
                except StopIteration:
                    return

        def stage1_halo_p1(bank):
            load("sp", xh_all, xTh_v, [("xa", 0)])
            act(sqh_all, xh_all, AF.Square, [("xa", 0)], [("xa", 1)])
            S.op("dve", lambda e: e.tensor_reduce(out=accss[:, 0:HALO],
                                                  in_=bass.AP(xa[1], 0, [[2 * T, 128], [1, HALO], [HALO, KC]]),
                                                  axis=AX.X, op=ALU.add), [("xa", 1)], ["accss"])
            mm(psum[bank][:, 0:HALO], [(ones_f[:], accss[:, 0:HALO])], ["ones_f", "accss"], [("ps", bank)])
            ts(lnr[:, 0:HALO], psum[bank][:, 0:HALO], 1.0 / D, EPS, ALU.mult, ALU.add, [], ["lnr", ("ps", bank)])
            rsqrt_inplace(lnr[:, 0:HALO], "lnr")

        def stage1_halo_p2():
            load("sp", xh_all, xTh_v, [("xa", 0)])
            for kc in range(KC):
                tmp, tk = gen()
                stt(tmp[:, 0:HALO], xh_kc(kc), gs[:, kc:kc + 1], lnr[:, 0:HALO], ALU.mult, ALU.mult,
                    [("xa", 0), "gs", "lnr"], [tk])
                act(hh[:, kc, :], tmp[:, 0:HALO], AF.Identity, [tk, "mod"], ["hh"], bias=shift[:, kc:kc + 1])

        def proj_group(b, piece, halo):
            bank = next_bank()
            pairs = [(slab[b][:, kc, piece * 128:(piece + 1) * 128], h[:, kc, :]) for kc in range(KC)]
            mm(psum[bank][:, 0:T], pairs, [("slab", b, piece), "hrd"], [("ps", bank)])
            hb = None
            if halo:
                hb = next_bank()
                pairs = [(slab[b][:, kc, piece * 128:(piece + 1) * 128], hh[:, kc, :]) for kc in range(KC)]
                mm(psum[hb][:, 0:HALO], pairs, [("slab", b, piece), "hh"], [("ps", hb)])
            return bank, hb

        def mixA_1(i, c, b_ac, h_ac, b_ax, h_ax):
            if i == 0:
                a, ak = gen()
                act(a[:, 0:HALO], psum[h_ac][:, 0:HALO], AF.Copy, [], [ak, ("ps", h_ac)])
                t_, tk = gen()
                tt(t_[:, 0:HALO], psum[h_ax][:, 0:HALO], a[:, 0:HALO], ALU.mult, [ak], [tk, ("ps", h_ax)])
                ts(cxc[:, c, :], t_[:, HALO - 2:HALO], hm[:, 0:1], None, ALU.mult, None, [tk, "hm"], [("cxc", c)])
            a, ak = gen()
            act(a[:], psum[b_ac][:, 0:T], AF.Copy, [], [ak, ("ps", b_ac)])
            cx = cxb[c % 2]
            ck = ("cx", c % 2)
            cp(cx[:, 0:2], cxc[:, c, :], [("cxc", c)], [ck])
            tt(cx[:, 2:T + 2], psum[b_ax][:, 0:T], a[:], ALU.mult, [ak], [ck, ("ps", b_ax)])
            cp(cxc[:, c, :], cx[:, T:T + 2], [ck], [("cxc", c)])
            co = convo[c % 2]
            cok = ("convo", c % 2)
            t_, tk = gen()
            w = lambda k: wA[:, c * CONV_A + k:c * CONV_A + k + 1]
            ts(co[:], cx[:, 0:T], w(0), None, ALU.mult, None, [ck, "wA"], [cok])
            stt(t_[:], cx[:, 1:T + 1], w(1), co[:], ALU.mult, ALU.add, [ck, "wA", cok], [tk])
            stt(co[:], cx[:, 2:T + 2], w(2), t_[:], ALU.mult, ALU.add, [ck, "wA", tk], [cok])

        def mixA_2(i, c, b_az, b_ab):
            sz, szk = gen()
            act(sz[:], psum[b_az][:, 0:T], AF.Silu, [], [szk, ("ps", b_az)])
            t4, t4k = gen()
            tt(t4[:], psum[b_ab][:, 0:T], convo[c % 2][:], ALU.mult, [("convo", c % 2)], [t4k, ("ps", b_ab)])
            tt(y[:, c, :], t4[:], sz[:], ALU.mult, [t4k, szk], [("y", c)])

        def mixB_1(i, c, b_bv, h_bv, b_bg, h_bg):
            if i == 0:
                th, thk = gen()
                act(th[:, 0:HALO], psum[h_bg][:, 0:HALO], AF.Tanh, [], [thk, ("ps", h_bg)], scale=0.5)
                u_, uk = gen()
                stt(u_[:, 0:HALO], th[:, 0:HALO], 1.0, psum[h_bv][:, 0:HALO], ALU.add, ALU.mult,
                    [thk], [uk, ("ps", h_bv)])
                ts(ucar[:, c, :], u_[:, HALO - 30:HALO], hm[:, 0:1], None, ALU.mult, None, [uk, "hm"], [("ucar", c)])
            th, thk = gen()
            act(th[:], psum[b_bg][:, 0:T], AF.Tanh, [], [thk, ("ps", b_bg)], scale=0.5)
            ub = ubuf[c % 2]
            ubk = ("ub", c % 2)
            cp(ub[:, 0:30], ucar[:, c, :], [("ucar", c)], [ubk])
            stt(ub[:, 30:T + 30], th[:], 1.0, psum[b_bv][:, 0:T], ALU.add, ALU.mult, [thk], [ubk, ("ps", b_bv)])
            cp(ucar[:, c, :], ub[:, T:T + 30], [ubk], [("ucar", c)])
            vk = ("big", c)
            bufs = [(vbuf(c), vk), (acc1[:], "acc1")]
            w = lambda k: wB[:, c * CONV_B + k:c * CONV_B + k + 1]
            ts(vbuf(c), ub[:, 0:T], w(0), bB[:, c:c + 1], ALU.mult, ALU.add, [ubk, "wB", "bB"], [vk])
            for k in range(1, CONV_B):
                o, ok = bufs[k % 2]
                p, pk = bufs[(k - 1) % 2]
                stt(o, ub[:, k:k + T], w(k), p, ALU.mult, ALU.add, [ubk, "wB", pk], [ok])

        def mixB_stats(c):
            if c == 0:
                cp(vsum[:], vbuf(c), [("big", c)], ["vsum"])
                act(qsum[:], vbuf(c), AF.Square, [("big", c)], ["qsum"])
            else:
                sq, sqk = gen()
                act(sq[:], vbuf(c), AF.Square, [("big", c)], [sqk])
                tt(vsum[:], vsum[:], vbuf(c), ALU.add, [("big", c)], ["vsum"])
                tt(qsum[:], qsum[:], sq[:], ALU.add, [sqk], ["qsum"])
            if c == CB - 1:
                mm(psum[PS_S1][:, 0:T], [(ones_f[:], vsum[:])], ["ones_f", "vsum"], [("ps", PS_S1)])
                mm(psum[PS_S2][:, 0:T], [(ones_f[:], qsum[:])], ["ones_f", "qsum"], [("ps", PS_S2)])

        def ln_finalize():
            mean, mk = gen()
            act(mean[:], psum[PS_S1][:, 0:T], AF.Copy, [], [mk, ("ps", PS_S1)], scale=1.0 / WA)
            msq, qk = gen()
            tt(msq[:], mean[:], mean[:], ALU.mult, [mk], [qk])
            stt(lnr[:], psum[PS_S2][:, 0:T], 1.0 / WA, msq[:], ALU.mult, ALU.subtract, [qk], ["lnr", ("ps", PS_S2)])
            ts(lnr[:], lnr[:], EPS, None, ALU.add, None, [], ["lnr"])
            rsqrt_inplace(lnr[:], "lnr")
            stt(lnm[:], mean[:], -1.0, lnr[:], ALU.mult, ALU.mult, [mk, "lnr"], ["lnm"])

        def mixB_2(c, b_bz):
            t1, k1 = gen()
            tt(t1[:], vbuf(c), lnr[:], ALU.mult, [("big", c), "lnr"], [k1])
            t2, k2 = gen()
            tt(t2[:], t1[:], lnm[:], ALU.add, [k1, "lnm"], [k2])
            s1, sk = gen()
            act(s1[:], t2[:], AF.Silu, [k2, "lng", "lnb"], [sk], scale=lng[:, c:c + 1], bias=lnb[:, c:c + 1])
            sz, szk = gen()
            act(sz[:], psum[b_bz][:, 0:T], AF.Silu, [], [szk, ("ps", b_bz)])
            tt(y[:, CA + c, :], s1[:], sz[:], ALU.mult, [sk, szk], [("y", CA + c)])

        def outproj_tile(i, s1g, npump):
            NB = T // 128
            order = [2, 3, 0, 1]
            rows = [i * T + jj * 128 for jj in range(NB)]
            for jj in (0, 1):
                load("sp", xr(jj), x_ap[rows[jj]:rows[jj] + 128, :], xr_keys(jj))
            for q in range(NQ):
                load("sp", gf[:, 0, :], gate_ap[:, q * 512:(q + 1) * 512], ["gf0"], reads=["gate_d"])
                load("sp", gf[:, 1, :], fg_t.ap()[:, q * 512:(q + 1) * 512], ["gf1"])
                if q == 0:
                    for jj in (2, 3):
                        load("sp", xr(jj), x_ap[rows[jj]:rows[jj] + 128, :], xr_keys(jj) + ["hrd"])
                banks = {jj: next_bank() for jj in order}
                for eh in range(2):
                    b = next_slab()
                    for jj in order:
                        tcol = jj * 128
                        pairs = [(y[:, eh * KH + e, tcol:tcol + 128],
                                  bass.AP(slab[b], e * 512, [[KC * 256, 128], [1, 512]])) for e in range(KH)]
                        mm(psum[banks[jj]][:, 0:512], pairs,
                           [("slab", b, 0), ("slab", b, 1)] + [("y", eh * KH + e) for e in range(KH)],
                           [("ps", banks[jj])],
                           start=(eh == 0), stop=(eh == 1))
                for jj in order:
                    kq = xr_key(jj, q)
                    xs = xr_slice(jj, q)
                    tmp, tk = gen()
                    tt(tmp[:], psum[banks[jj]][:, 0:512], gf[:, 0, :], ALU.mult, ["gf0"], [tk, ("ps", banks[jj])])
                    tt(xs, tmp[:], xs, ALU.add, [tk], [kq])
                    sq, sqk = gen()
                    act(sq[:], xs, AF.Square, [kq], [sqk, ("ssq", jj)], accum_out=ssq[:, jj, q:q + 1])
                    tt(xs, xs, gf[:, 1, :], ALU.mult, ["gf1"], [kq])
                pump(s1g, npump)
            pump(s1g, 10 ** 6)
            def rstd_pair(j0):
                key = "sst%d" % j0
                for jj in (j0, j0 + 1):
                    S.op("dve", lambda e, jj=jj: e.tensor_reduce(out=sst[:, jj:jj + 1], in_=ssq[:, jj, 0:NQ],
                                                                axis=AX.X, op=ALU.add), [("ssq", jj)], [key])
                ts(sst[:, j0:j0 + 2], sst[:, j0:j0 + 2], 1.0 / D, EPS, ALU.mult, ALU.add, [], [key])
                rsqrt_inplace(sst[:, j0:j0 + 2], key)

            def fin(jj):
                if jj >= 2 and i + 1 < NT:
                    nh = 2 if NQ >= 2 else 1
                    per = NQ // nh
                    for hh_ in range(nh):
                        qs = list(range(hh_ * per, (hh_ + 1) * per))
                        for q in qs:
                            xs = xr_slice(jj, q)
                            ts(xs, xs, sst[:, jj:jj + 1], None, ALU.mult, None, ["sst2"], [xr_key(jj, q)])
                        c0, c1 = qs[0] * 512, (qs[-1] + 1) * 512
                        src = xr(jj)[:, c0:c1] if False else (hx[:, (jj - 2) * D + c0:(jj - 2) * D + c1])
                        S.dma("sp", lambda e, jj=jj, c0=c0, c1=c1, src=src:
                              e.dma_start(out=out_ap[rows[jj]:rows[jj] + 128, c0:c1], in_=src),
                              reads=[xr_key(jj, q) for q in qs], is_output=True)
                    return
                act(xr(jj), xr(jj), AF.Copy, ["sst%d" % (2 * (jj // 2))], xr_keys(jj), scale=sst[:, jj:jj + 1])
                S.dma("act", lambda e, jj=jj: e.dma_start(out=out_ap[rows[jj]:rows[jj] + 128, :], in_=xr(jj)),
                      reads=xr_keys(jj), is_output=True)

            pre = stage1_p2_prefetch(i + 1, NGEN - 1) if i + 1 < NT else ()
            rstd_pair(2)
            fin(2)
            fin(3)
            if i + 1 < NT:
                stage1_p2(i + 1, pre)
            rstd_pair(0)
            fin(0)
            fin(1)

        prologue()
        stage1_halo_p2()
        stage1_p2(0)
        for i in range(NT):
            halo = (i == 0)
            pend_stats = None
            for c in range(CA):
                b = next_slab()
                b_ac, h_ac = proj_group(b, 0, halo)
                b_ax, h_ax = proj_group(b, 1, halo)
                mixA_1(i, c, b_ac, h_ac, b_ax, h_ax)
                b = next_slab()
                b_az, _ = proj_group(b, 0, False)
                b_ab, _ = proj_group(b, 1, False)
                mixA_2(i, c, b_az, b_ab)
                b = next_slab()
                b_bv, h_bv = proj_group(b, 0, halo)
                b_bg, h_bg = proj_group(b, 1, halo)
                if pend_stats is not None:
                    mixB_stats(pend_stats)
                mixB_1(i, c, b_bv, h_bv, b_bg, h_bg)
                pend_stats = c
            bz_banks = {}
            for c2 in range(CB // 2):
                b = next_slab()
                for s in range(2):
                    bz_banks[2 * c2 + s], _ = proj_group(b, s, False)
                if c2 == 0:
                    mixB_stats(pend_stats)
                    ln_finalize()
                for s in range(2):
                    mixB_2(2 * c2 + s, bz_banks[2 * c2 + s])
            s1g = stage1_p1(i + 1) if i + 1 < NT else None
            outproj_tile(i, s1g, -(-(KC // 2 + 1) // NQ))
        S.finish("act")
        with nc.Block() as block:
            S.emit(block)
    return nc


def make_in_maps(x, c, norm_g, w_ada, b_ada, w_in, conv_a_w, conv_b_w, conv_b_b, ln_b_g, ln_b_b, w_out,
                 final_g, n_cores):
    B, SEQ, D = x.shape
    per_b = n_cores // B
    NTOK = SEQ // per_b
    KC = D // 128
    WA = D // 2
    CA = WA // 128
    f32 = np.float32

    def pp(v, n):
        return np.ascontiguousarray(np.asarray(v, f32).reshape(n, 128).T)

    w_ada0 = np.ascontiguousarray(np.asarray(w_ada[0], f32))
    w_in0 = np.ascontiguousarray(np.asarray(w_in[0], f32))
    w_out0 = np.ascontiguousarray(np.asarray(w_out[0], f32))
    b_ada0 = np.asarray(b_ada[0], f32)
    shared = {
        "w_ada": w_ada0, "w_in": w_in0, "w_out": w_out0,
        "b_ss_pp": pp(b_ada0[0:2 * D], 2 * KC),
        "bgate_bc": np.ascontiguousarray(np.broadcast_to(b_ada0[2 * D:3 * D][None, :], (128, D))),
        "ng_pp": pp(norm_g[0], KC),
        "wA_pp": np.ascontiguousarray(np.asarray(conv_a_w[0], f32).reshape(CONV_A, CA, 128).transpose(2, 1, 0)
                                      .reshape(128, CA * CONV_A)),
        "wB_pp": np.ascontiguousarray(np.asarray(conv_b_w[0], f32).reshape(CONV_B, CA, 128).transpose(2, 1, 0)
                                      .reshape(128, CA * CONV_B)),
        "bB_pp": pp(conv_b_b[0], CA), "lng_pp": pp(ln_b_g[0], CA), "lnb_pp": pp(ln_b_b[0], CA),
        "fg_bc": np.ascontiguousarray(np.broadcast_to(np.asarray(final_g, f32)[None, :], (128, D))),
    }
    in_maps = []
    for core in range(n_cores):
        b, half = divmod(core, per_b)
        s0 = half * NTOK
        xs = np.asarray(x[b, s0:s0 + NTOK, :], f32)
        if s0 >= HALO:
            xh = np.asarray(x[b, s0 - HALO:s0, :], f32)
            mask = 1.0
        else:
            xh = np.zeros((HALO, D), f32)
            mask = 0.0
        m = dict(shared)
        m["x"] = np.ascontiguousarray(xs)
        m["xT"] = np.ascontiguousarray(xs.T)
        m["xTh"] = np.ascontiguousarray(xh.T)
        m["c_pp"] = pp(c[b], KC)
        m["hmask"] = np.full((128, 1), mask, f32)
        in_maps.append(m)
    return in_maps, NTOK


_CACHE = {}


def kernel(x, c, norm_g, w_ada, b_ada, w_in, conv_a_w, conv_b_w, conv_b_b, ln_b_g, ln_b_b, w_out, final_g,
           n_cores=8):
    x = np.asarray(x)
    B, SEQ, D = x.shape
    in_maps, NTOK = make_in_maps(x, np.asarray(c), np.asarray(norm_g), np.asarray(w_ada), np.asarray(b_ada),
                                 np.asarray(w_in), np.asarray(conv_a_w), np.asarray(conv_b_w),
                                 np.asarray(conv_b_b), np.asarray(ln_b_g), np.asarray(ln_b_b),
                                 np.asarray(w_out), np.asarray(final_g), n_cores)
    key = (D, NTOK)
    if key not in _CACHE:
        _CACHE[key] = build_program(D, NTOK)
    nc = _CACHE[key]
    res = run_bass_kernel_spmd(nc, in_maps, core_ids=list(range(n_cores)))
    per_b = n_cores // B
    out = np.empty((B, SEQ, D), np.float32)
    for core in range(n_cores):
        b, half = divmod(core, per_b)
        out[b, half * NTOK:(half + 1) * NTOK, :] = res.results[core]["out"]
    return out
```
